# Optimizing a Trainium2 kernel written in Bass

```python
import math
import jax, jax.numpy as jnp
from jax import lax
import numpy as np

D_MODEL = 1024
BATCH = 8
SEQ = 2048
DEPTH = 2

HEAD_DIM = 64
FOX_HEADS = 8
DIFF_HEADS = 4
FOX_WIDTH = FOX_HEADS * HEAD_DIM
DIFF_WIDTH = DIFF_HEADS * 2 * HEAD_DIM
MIX_WIDTH = FOX_WIDTH + DIFF_WIDTH
IN_WIDTH = 3 * FOX_WIDTH + FOX_HEADS + 3 * DIFF_WIDTH
D_FF = ((8 * D_MODEL // 3 + 255) // 256) * 256
BLOCK_Q = 128
EPS = 1e-6

kernel_name = "fox_diffattn_hybrid_adaln"


def _rms(x, g):
    xf = x.astype(jnp.float32)
    y = xf * lax.rsqrt(jnp.mean(xf * xf, axis=-1, keepdims=True) + EPS)
    return (y * g.astype(jnp.float32)).astype(x.dtype)


def _to_blocks(t):
    b, h, s = t.shape[:3]
    nb = s // BLOCK_Q
    t = t.reshape((b, h, nb, BLOCK_Q) + t.shape[3:])
    return jnp.moveaxis(t, 2, 0)


def _from_blocks(t):
    nb, b, h, bq, d = t.shape
    return jnp.moveaxis(t, 0, 2).reshape(b, h, nb * bq, d)


def _fox_attention(q, k, v, log_f):
    s_len = q.shape[2]
    scale = 1.0 / math.sqrt(q.shape[-1])
    cum = jnp.cumsum(log_f, axis=-1)
    key_pos = jnp.arange(s_len)
    starts = jnp.arange(s_len // BLOCK_Q) * BLOCK_Q

    def one_block(args):
        qi, ci, st = args
        s = jnp.einsum('bhqd,bhkd->bhqk', qi, k, preferred_element_type=jnp.float32) * scale
        s = s + ci[..., :, None] - cum[..., None, :]
        qpos = st + jnp.arange(BLOCK_Q)
        mask = key_pos[None, :] <= qpos[:, None]
        p = jax.nn.softmax(jnp.where(mask, s, -jnp.inf), axis=-1)
        return jnp.einsum('bhqk,bhkd->bhqd', p.astype(v.dtype), v)

    out = lax.map(one_block, (_to_blocks(q), _to_blocks(cum), starts))
    return _from_blocks(out)


def _diff_attention(q1, q2, k1, k2, v, lam, slopes):
    s_len = q1.shape[2]
    scale = 1.0 / math.sqrt(q1.shape[-1])
    key_pos = jnp.arange(s_len)
    starts = jnp.arange(s_len // BLOCK_Q) * BLOCK_Q

    def one_block(args):
        q1i, q2i, st = args
        qpos = st + jnp.arange(BLOCK_Q)
        dist = (qpos[:, None] - key_pos[None, :]).astype(jnp.float32)
        alibi = -slopes[:, None, None] * dist
        mask = dist >= 0
        s1 = jnp.einsum('bhqd,bhkd->bhqk', q1i, k1, preferred_element_type=jnp.float32) * scale + alibi
        s2 = jnp.einsum('bhqd,bhkd->bhqk', q2i, k2, preferred_element_type=jnp.float32) * scale + alibi
        p = (jax.nn.softmax(jnp.where(mask, s1, -jnp.inf), axis=-1)
             - lam * jax.nn.softmax(jnp.where(mask, s2, -jnp.inf), axis=-1))
        return jnp.einsum('bhqk,bhkd->bhqd', p.astype(v.dtype), v)

    out = lax.map(one_block, (_to_blocks(q1), _to_blocks(q2), starts))
    return _from_blocks(out)


def _modulate(x, g, shift, scale):
    return _rms(x, g) * (1.0 + scale[:, None, :]) + shift[:, None, :]


def setup_inputs(seed: int = 0) -> dict:
    key = jax.random.key(seed)
    ks = jax.random.split(key, 20)
    f32 = jnp.float32
    nrm = lambda k, shape, s: jax.random.normal(k, shape, f32) * s
    d = D_MODEL
    return {
        "x": nrm(ks[0], (BATCH, SEQ, d), 1.0),
        "c": nrm(ks[1], (BATCH, d), 1.0),
        "ln1_g": 1.0 + nrm(ks[2], (DEPTH, d), 0.02),
        "ln2_g": 1.0 + nrm(ks[3], (DEPTH, d), 0.02),
        "w_ada": nrm(ks[4], (DEPTH, d, 6 * d), 0.5 * d ** -0.5),
        "b_ada": nrm(ks[5], (DEPTH, 6 * d), 0.02),
        "w_in": nrm(ks[6], (DEPTH, d, IN_WIDTH), d ** -0.5),
        "b_f": 1.5 + nrm(ks[7], (DEPTH, FOX_HEADS), 0.1),
        "fox_qk_g": 1.0 + nrm(ks[8], (DEPTH, 2, HEAD_DIM), 0.02),
        "diff_qk_g": 1.0 + nrm(ks[9], (DEPTH, 2, HEAD_DIM), 0.02),
        "diff_lam": nrm(ks[10], (DEPTH, 4, HEAD_DIM), 0.1),
        "diff_norm_g": 1.0 + nrm(ks[11], (DEPTH, 2 * HEAD_DIM), 0.02),
        "w_out": nrm(ks[12], (DEPTH, MIX_WIDTH, d), MIX_WIDTH ** -0.5),
        "w_gate": nrm(ks[13], (DEPTH, d, D_FF), d ** -0.5),
        "w_up": nrm(ks[14], (DEPTH, d, D_FF), d ** -0.5),
        "w_down": nrm(ks[15], (DEPTH, D_FF, d), D_FF ** -0.5),
    }


def reference(x, c, ln1_g, ln2_g, w_ada, b_ada, w_in, b_f, fox_qk_g, diff_qk_g,
              diff_lam, diff_norm_g, w_out, w_gate, w_up, w_down):
    b, s, d = x.shape
    cond = jax.nn.silu(c)
    slopes = 2.0 ** (-8.0 * jnp.arange(1, DIFF_HEADS + 1, dtype=jnp.float32) / DIFF_HEADS)
    splits = np.cumsum([FOX_WIDTH, FOX_WIDTH, FOX_WIDTH, FOX_HEADS,
                        DIFF_WIDTH, DIFF_WIDTH]).tolist()

    for l in range(DEPTH):
        mod = cond @ w_ada[l] + b_ada[l]
        sh1, sc1, g1, sh2, sc2, g2 = jnp.split(mod, 6, axis=-1)

        h = _modulate(x, ln1_g[l], sh1, sc1)
        u = h @ w_in[l]
        fq, fk, fv, fg, dq, dk, dv = jnp.split(u, splits, axis=-1)

        heads = lambda t, n, hd: t.reshape(b, s, n, hd).transpose(0, 2, 1, 3)
        fq = _rms(heads(fq, FOX_HEADS, HEAD_DIM), fox_qk_g[l, 0])
        fk = _rms(heads(fk, FOX_HEADS, HEAD_DIM), fox_qk_g[l, 1])
        fv = heads(fv, FOX_HEADS, HEAD_DIM)
        log_f = jax.nn.log_sigmoid(fg.astype(jnp.float32) + b_f[l].astype(jnp.float32))
        log_f = log_f.transpose(0, 2, 1)
        fox_out = _fox_attention(fq, fk, fv, log_f)

        dq = dq.reshape(b, s, DIFF_HEADS, 2, HEAD_DIM).transpose(0, 2, 3, 1, 4)
        dk = dk.reshape(b, s, DIFF_HEADS, 2, HEAD_DIM).transpose(0, 2, 3, 1, 4)
        dq = _rms(dq, diff_qk_g[l, 0])
        dk = _rms(dk, diff_qk_g[l, 1])
        dv = heads(dv, DIFF_HEADS, 2 * HEAD_DIM)
        lam_init = 0.8 - 0.6 * math.exp(-0.3 * l)
        lv = diff_lam[l].astype(jnp.float32)
        lam = jnp.exp(jnp.sum(lv[0] * lv[1])) - jnp.exp(jnp.sum(lv[2] * lv[3])) + lam_init
        diff_out = _diff_attention(dq[:, :, 0], dq[:, :, 1], dk[:, :, 0], dk[:, :, 1], dv, lam, slopes)
        diff_out = _rms(diff_out, diff_norm_g[l]) * (1.0 - lam_init)

        mixed = jnp.concatenate([
            fox_out.transpose(0, 2, 1, 3).reshape(b, s, FOX_WIDTH),
            diff_out.transpose(0, 2, 1, 3).reshape(b, s, DIFF_WIDTH)], axis=-1)
        x = x + g1[:, None, :] * (mixed @ w_out[l])

        h2 = _modulate(x, ln2_g[l], sh2, sc2)
        y = (jax.nn.silu(h2 @ w_gate[l]) * (h2 @ w_up[l])) @ w_down[l]
        x = x + g2[:, None, :] * y
    return x
```

```python
import math
import collections
import contextlib
import numpy as np
import concourse.bass as bass
import concourse.mybir as mybir
from concourse.bass_utils import run_bass_kernel_spmd

F32 = mybir.dt.float32
BF16 = mybir.dt.bfloat16
AF = mybir.ActivationFunctionType
ALU = mybir.AluOpType
AX = mybir.AxisListType

D = 1024
S = 2048
DEPTH = 2
DFF = 2816
NFC = DFF // 128
INW = 3080
EPS = 1e-6
NP_ = 8 + 8 + 48 + 4 + 256 + 128 + 1
O_LN1, O_LN2, O_BADA, O_QKG, O_LAM, O_GN, O_BF = 0, 8, 16, 64, 68, 324, 452

ENGS = ("pe", "act", "dve", "pool", "sp")


class Prog:
    def __init__(self, nc):
        self.nc = nc
        self.ops = []
        self.lastw = {}
        self.rd_c = {}
        self.rd_d = {}
        self.ecount = {e: 0 for e in ENGS}
        self.pending = {e: set() for e in ENGS}

    def barrier(self, engines=ENGS):
        lastc = {}
        lastd = {"sp": [], "pool": []}
        for op in self.ops:
            if op["dma"]:
                lastd[op["eng"]].append(op["idx"])
            else:
                lastc[op["eng"]] = op["idx"]
        deps = set(lastc.values())
        for q in lastd:
            deps.update(lastd[q][-8:])
        for e in engines:
            self.pending[e] = set(deps)

    def add(self, eng, fn, r=(), w=(), dma=False):
        idx = len(self.ops)
        deps = set()
        if self.pending[eng]:
            deps |= self.pending[eng]
            self.pending[eng] = set()
        for k in r:
            if k in self.lastw:
                deps.add(self.lastw[k])
        for k in w:
            if k in self.lastw:
                deps.add(self.lastw[k])
            for v in self.rd_c.get(k, {}).values():
                deps.add(v)
            for v in self.rd_d.get(k, ()):
                deps.add(v)
        for k in w:
            self.lastw[k] = idx
            self.rd_c[k] = {}
            self.rd_d[k] = []
        ws = set(w)
        for k in r:
            if k in ws:
                continue
            if dma:
                self.rd_d.setdefault(k, []).append(idx)
            else:
                self.rd_c.setdefault(k, {})[eng] = idx
        deps.discard(idx)
        self.ops.append(dict(eng=eng, fn=fn, dma=dma, idx=idx, eidx=self.ecount[eng], deps=deps, ms=False))
        self.ecount[eng] += 1
        return idx

    def emit(self, sems, dsems):
        nc = self.nc
        H = {"pe": nc.tensor, "act": nc.scalar, "dve": nc.vector, "pool": nc.gpsimd, "sp": nc.sync}
        ops = self.ops
        dcnt = {q: 0 for q in dsems}
        dtot = {}
        for op in ops:
            if op["dma"]:
                q = op["eng"]
                pool = dsems[q]
                si = dcnt[q] % len(pool)
                dcnt[q] += 1
                prev = dtot.get((q, si), 0)
                op["dsem"] = pool[si]
                op["dkey"] = (q, si)
                op["dprev"] = prev
                op["dval"] = prev + 16
                dtot[(q, si)] = prev + 16
        for op in ops:
            for di in op["deps"]:
                y = ops[di]
                if y["dma"]:
                    continue
                if y["eng"] != op["eng"]:
                    y["ms"] = True
                elif op["eng"] != "pe" and (op["eidx"] - y["eidx"] <= 3):
                    y["ms"] = True
        mcount = {e: 0 for e in ENGS}
        for op in ops:
            if op["ms"]:
                mcount[op["eng"]] += 1
                op["msval"] = mcount[op["eng"]]
        waited = {e: {} for e in ENGS}
        nwait = 0
        for op in ops:
            e = op["eng"]
            h = H[e]
            need = {}
            for di in op["deps"]:
                y = ops[di]
                if y["dma"]:
                    key = ("d",) + y["dkey"]
                    need[key] = max(need.get(key, 0), y["dval"])
                else:
                    if y["eng"] == e and (e == "pe" or op["eidx"] - y["eidx"] > 3):
                        continue
                    key = ("c", y["eng"])
                    need[key] = max(need.get(key, 0), y["msval"])
            if op["dma"] and op["dprev"] > 0:
                key = ("d",) + op["dkey"]
                need[key] = max(need.get(key, 0), op["dprev"])
            for key, val in need.items():
                if waited[e].get(key, 0) >= val:
                    continue
                waited[e][key] = val
                sem = sems[key[1]] if key[0] == "c" else dsems[key[1]][key[2]]
                h.wait_ge(sem, val)
                nwait += 1
            ins = op["fn"](h)
            if op["dma"]:
                ins.then_inc(op["dsem"], 16)
            elif op["ms"]:
                ins.then_inc(sems[e], 1)
        self.nwait = nwait
        return ops


def build(depth=DEPTH):
    nc = bass.Bass("TRN2", target_bir_lowering=False)

    def din(name, shape):
        return nc.dram_tensor(name, shape, F32, kind="ExternalInput").ap()

    xT_d = din("xT", [D, S])
    cT_d = din("cT", [128, 8])
    pp_d = din("pp", [DEPTH, 128, NP_])
    cst_d = din("cst", [128, 512])
    alibi_d = din("alibi", [4, 4, S])
    onesrow_d = din("onesrow", [8, S])
    wada_d = din("wada", [DEPTH, 24, 128, 2048])
    winp_d = din("winp", [DEPTH, 8, 128, 3072])
    wfg_d = din("wfg", [DEPTH, 128, 64])
    w_out_d = din("w_out", [DEPTH, D, D])
    wgu_d = din("wgu", [DEPTH, 11, 128, 4096])
    wdn_d = din("wdn", [DEPTH, 8, 128, DFF])
    outT_d = nc.dram_tensor("outT", [D, S], F32, kind="ExternalOutput").ap()
    cums_d = nc.dram_tensor("cums", [8, 4, S], BF16, kind="Internal").ap()
    alsp_d = nc.dram_tensor("alsp", [4, 4, S], BF16, kind="Internal").ap()

    es = contextlib.ExitStack()
    with es:
        def sb(name, shape, dt):
            return es.enter_context(nc.sbuf_tensor(name, shape, dt))

        xT = sb("xT_sb", [128, 8, S], F32)
        hT = sb("hT", [128, 8, S], BF16)
        WS = [sb(f"ws{i}", [128, 4096], BF16) for i in range(3)]
        MW = [sb(f"mw{i}", [128, 2048], BF16) for i in range(2)]
        RCOLS = 30848
        R = sb("R", [128, RCOLS], BF16)
        o = 0
        AUG = []
        for s_ in range(2):
            q_ = R[:, o:o + 2048]; o += 2048
            k_ = R[:, o:o + 2048]; o += 2048
            AUG.append((q_, k_))
        pairq = R[:, o:o + 2048]; o += 2048
        pairk = R[:, o:o + 2048]; o += 2048
        Vf = R[:, o:o + 16 * 130].rearrange("p (t h d) -> p t h d", t=16, h=2); o += 2112
        Vd = R[:, o:o + 16 * 129].rearrange("p (t d) -> p t d", t=16); o += 2112
        PTall = R[:, o:o + 2048]; o += 2048
        PT = [PTall[:, i * 512:(i + 1) * 512] for i in range(4)]
        mixed = R[:, o:o + 8192].rearrange("p (t c) -> p t c", t=16); o += 8192
        cumt1_b = R[:, o:o + 4096]; o += 4096
        cumv = cumt1_b.bitcast(F32)
        t1 = cumv.rearrange("p (t d) -> p t d", t=16)
        att_end = o
        o = 0
        actT = R[:, o:o + NFC * 1024].rearrange("p (c t) -> p c t", c=NFC); o += NFC * 1024
        WD = []
        for i in range(2):
            WD.append(R[:, o:o + NFC * 128]); o += NFC * 128
        SG = []
        for i in range(2):
            SG.append(R[:, o:o + 1024].bitcast(F32)); o += 1024
        assert max(att_end, o) <= RCOLS, (att_end, o)
        WD_OVERLAP = [("mixed", t) for t in range(16)] + ["cum"]
        FFN_KEYS = [("actT", c) for c in range(NFC)] + [("wd", 0), ("wd", 1), ("sg", 0), ("sg", 1)]

        tmpA = sb("tmpA", [128, 512], F32)
        tmpB = sb("tmpB", [128, 512], F32)
        TMP = [tmpA, tmpB]
        XN = [sb(f"xn{i}", [128, 512], F32) for i in range(2)]
        SQ = [sb(f"sq{i}", [128, 512], BF16) for i in range(2)]
        cst = sb("cst_bf", [128, 512], BF16)
        ident = cst[:, 0:128]
        maskb = cst[:, 128:256]
        bdones = cst[:, 256:384]
        ones128 = cst[:, 384:512]
        zt = sb("zt", [128, 512], BF16)
        c_sb = sb("c_sb", [128, 8], F32)
        condT = sb("condT", [128, 8], BF16)
        pp = sb("pp_sb", [128, DEPTH, NP_], F32)
        modT = [sb(f"modT{l}", [128, 48], F32) for l in range(DEPTH)]
        a1 = [sb(f"a1_{l}", [128, 8], F32) for l in range(DEPTH)]
        a2 = [sb(f"a2_{l}", [128, 8], F32) for l in range(DEPTH)]
        qkgs = sb("qkgs", [128, 4], F32)
        negbf = sb("negbf", [8, 1], F32)
        ones8 = sb("ones8", [8, 1], F32)
        neglam = sb("neglam", [128, 1], F32)
        lamt = sb("lamt", [128, 4], F32)
        lamp = sb("lamp", [128, 64], F32)
        gnb = sb("gnb", [128, 128], F32)
        rc = sb("rc", [128, 8], F32)
        ssd = sb("ssd", [128, 8], F32)
        junk = sb("junk", [128, 128], BF16)
        mhalf = sb("mhalf", [128, 4], F32)
        wfg = sb("wfg_sb", [128, 8, 8], BF16)

        PS = [es.enter_context(nc.psum_tensor(f"ps{i}", [128, 512], F32)) for i in range(8)]

        sems = {e: es.enter_context(nc.semaphore(f"sem_{e}")) for e in ENGS}
        dsems = {
            "sp": [es.enter_context(nc.semaphore(f"dsp{i}")) for i in range(8)],
            "pool": [es.enter_context(nc.semaphore(f"dpl{i}")) for i in range(8)],
        }

        P = Prog(nc)
        wctr = [0]
        mwctr = [0]
        BAR_ENGS = ("pe", "act", "dve", "sp")

        def wslot():
            i = wctr[0] % 3
            wctr[0] += 1
            return i

        def psk(b):
            return ("ps", b)

        def mk(l, j0, j1):
            return [("modT", l, b) for b in range(j0 // 2, (j1 + 1) // 2)]

        P.add("sp", lambda h: h.dma_start(out=c_sb[:], in_=cT_d[:]), w=["c_sb"], dma=True)
        P.add("sp", lambda h: h.dma_start(out=pp[:], in_=pp_d.rearrange("l p n -> p l n")), w=["pp"], dma=True)
        def x_load(tg, q):
            P.add(q, lambda h, tg=tg: h.dma_start(
                out=xT[:, :, tg * 512:(tg + 1) * 512],
                in_=xT_d[:, tg * 512:(tg + 1) * 512].rearrange("(k p) t -> p k t", p=128)),
                w=[("xT", tg)], dma=True)

        x_load(0, "sp")
        P.add("pool", lambda h: h.dma_start(out=cst[:], in_=cst_d[:]), w=["cst"], dma=True)
        P.add("dve", lambda h: h.memset(zt[:], 0.0), w=["zt"])
        P.add("dve", lambda h: h.memset(ones8[:], 1.0), w=["ones8"])
        P.add("dve", lambda h: h.memset(mhalf[:], -0.5), w=["mhalf"])
        P.add("act", lambda h: h.activation(out=condT[:], in_=c_sb[:], func=AF.Silu), r=["c_sb"], w=["condT"])

        ptkeys = [("pt", i) for i in range(4)]

        def split_rows(nrows, dst_d, chunks=range(4)):
            spst = PTall[0:nrows, 0:2048].rearrange("p (a b) -> p a b", a=4)
            for c in chunks:
                xs = cumv[0:nrows, c * 512:(c + 1) * 512]
                for a in range(4):
                    P.add("dve", lambda h, xs=xs, a=a: h.tensor_copy(out=spst[:, a, :], in_=xs),
                          r=["cum"], w=ptkeys)
                    if a < 3:
                        P.add("dve", lambda h, xs=xs, a=a: h.tensor_tensor(out=xs, in0=xs, in1=spst[:, a, :], op=ALU.subtract),
                              r=ptkeys, w=["cum"])
                P.add("sp", lambda h, c=c: h.dma_start(out=dst_d[:, :, c * 512:(c + 1) * 512], in_=spst),
                      r=ptkeys, w=[("splitd", id(dst_d))], dma=True)

        def mod_dma(l, blk):
            mi = mwctr[0] % 2
            mwctr[0] += 1
            mw = MW[mi]
            P.add("pool", lambda h, mw=mw, blk=blk: h.dma_start(out=mw[:], in_=wada_d[l, blk]),
                  w=[("mw", mi)], dma=True)
            return mi

        def mod_blocks(l, blks, mis=None):
            for bi, blk in enumerate(blks):
                mi = mis[bi] if mis is not None else mod_dma(l, blk)
                mod_compute(l, blk, MW[mi][:], ("mw", mi))

        def mod_compute(l, blk, mwap, mwkey):
            if True:
                mi = mwkey
                wv = mwap.rearrange("p (k c) -> p k c", k=8)
                for cc in range(2):
                    for kc in range(8):
                        P.add("pe", lambda h, wv=wv, cc=cc, kc=kc: h.matmul(
                            PS[7][:, cc:cc + 1], wv[:, kc, cc * 128:(cc + 1) * 128], condT[:, kc:kc + 1],
                            start=(kc == 0), stop=(kc == 7)),
                            r=[mwkey, "condT"], w=[psk(7)])
                P.add("dve", lambda h, blk=blk: h.tensor_tensor(
                    out=modT[l][:, 2 * blk:2 * blk + 2], in0=PS[7][:, 0:2],
                    in1=pp[:, l, O_BADA + 2 * blk:O_BADA + 2 * blk + 2], op=ALU.add),
                    r=[psk(7), "pp"], w=[("modT", l, blk)])

        def mod_a(l, which):
            if which == 1:
                P.add("dve", lambda h: h.scalar_tensor_tensor(out=a1[l][:], in0=modT[l][:, 8:16], scalar=1.0,
                                                             in1=pp[:, l, O_LN1:O_LN1 + 8], op0=ALU.add, op1=ALU.mult),
                      r=mk(l, 8, 16) + ["pp"], w=[("a1", l)])
            else:
                P.add("dve", lambda h: h.scalar_tensor_tensor(out=a2[l][:], in0=modT[l][:, 32:40], scalar=1.0,
                                                             in1=pp[:, l, O_LN2:O_LN2 + 8], op0=ALU.add, op1=ALU.mult),
                      r=mk(l, 32, 40) + ["pp"], w=[("a2", l)])

        def prologue():
            x_load(1, "sp")
            bufs = [(MW[0][:], ("mw", 0)), (MW[1][:], ("mw", 1))]
            for i_ in range(3):
                bufs.append((WS[i_][:, 0:2048], ("w", i_)))
                bufs.append((WS[i_][:, 2048:4096], ("w", i_)))
            for blk in range(8):
                P.add("pool", lambda h, blk=blk: h.dma_start(out=bufs[blk][0], in_=wada_d[0, blk]),
                      w=[bufs[blk][1]], dma=True)
                if blk == 3:
                    x_load(2, "pool")
            x_load(3, "pool")
            P.add("pool", lambda h: h.dma_start(out=alsp_d, in_=alibi_d), w=[("splitd", id(alsp_d))], dma=True)
            rms_p1(0)
            rms_p1(1)
            for blk in range(8):
                mod_compute(0, blk, bufs[blk][0], bufs[blk][1])
            mod_a(0, 1)
            rms_p2(0, 0, a1[0], ("a1", 0), 0)
            rms_p1(2)
            rms_p2(0, 1, a1[0], ("a1", 0), 0)
            rms_p1(3)
            rms_p2(0, 2, a1[0], ("a1", 0), 0)
            rms_p2(0, 3, a1[0], ("a1", 0), 0)

        def rms_p1(tg):
            tsl = slice(tg * 512, (tg + 1) * 512)
            tk = tg % 2
            T = TMP[tk]
            for kc in range(8):
                wk = [("hTs", tg, kc)] + ([("hT", tg), ("hTg", tg)] if kc == 0 else [])
                rk = [("xT", tg)] + ([] if kc == 0 else [("hTg", tg)])
                if kc not in (1, 5):
                    P.add("act", lambda h, kc=kc: h.activation(out=hT[:, kc, tsl], in_=xT[:, kc, tsl], func=AF.Square),
                          r=rk, w=wk)
                else:
                    P.add("dve", lambda h, kc=kc: h.tensor_tensor(out=hT[:, kc, tsl], in0=xT[:, kc, tsl], in1=xT[:, kc, tsl], op=ALU.mult),
                          r=rk, w=wk)
            for kc in range(8):
                P.add("pe", lambda h, kc=kc: h.matmul(PS[7][:], ones128, hT[:, kc, tsl], start=(kc == 0), stop=(kc == 7)),
                      r=[("hTs", tg, kc), "cst"], w=[psk(7)])
            P.add("act", lambda h: h.activation(out=T[:], in_=PS[7][:], func=AF.Ln, scale=1.0 / D, bias=EPS),
                  r=[psk(7)], w=[("tmp", tk)])
            P.add("act", lambda h: h.activation(out=T[:], in_=T[:], func=AF.Exp, scale=-0.5),
                  r=[("tmp", tk)], w=[("tmp", tk)])

        def rms_p2_thunks(l, tg, a_t, akey, sh_col0):
            tsl = slice(tg * 512, (tg + 1) * 512)
            tk = tg % 2
            T = TMP[tk]
            shkeys = mk(l, sh_col0, sh_col0 + 8)
            th = []
            for kc in range(8):
                def one(kc=kc):
                    xn = XN[kc % 2]
                    P.add("dve", lambda h: h.scalar_tensor_tensor(
                        out=xn[:], in0=xT[:, kc, tsl], scalar=a_t[:, kc:kc + 1], in1=T[:], op0=ALU.mult, op1=ALU.mult),
                        r=[("xT", tg), ("tmp", tk), akey], w=[("xn", kc % 2)])
                    P.add("act", lambda h: h.activation(
                        out=hT[:, kc, tsl], in_=xn[:], func=AF.Identity, bias=modT[l][:, sh_col0 + kc:sh_col0 + kc + 1]),
                        r=[("xn", kc % 2)] + shkeys, w=[("hT", tg), ("hTs", tg, kc)])
                th.append(one)
            return th

        def rms_p2(l, tg, a_t, akey, sh_col0):
            for t_ in rms_p2_thunks(l, tg, a_t, akey, sh_col0):
                t_()

        def rms_seq(l, tgs, a_t, akey, sh_col0):
            tgs = list(tgs)
            rms_p1(tgs[0])
            for i_, tg in enumerate(tgs):
                if i_ + 1 < len(tgs):
                    rms_p1(tgs[i_ + 1])
                rms_p2(l, tg, a_t, akey, sh_col0)

        pvctr = [0]

        def attention_units(units):
            tasks = [(u, j, kb) for u in range(len(units)) for j in range(4) for kb in range(4 * j + 4)]
            LAG = 3
            SB = (0, 1, 2, 7)
            n = len(tasks)
            info = {}
            for s_ in range(n + LAG):
                if s_ < n:
                    u, j, kb = tasks[s_]
                    aslot = units[u][0]
                    Qa, Ka = AUG[aslot]
                    off = max(0, kb - 4 * j) * 128
                    sbk = SB[s_ % 4]
                    pts = s_ % 4
                    info[s_] = (sbk, pts, off)
                    diag = kb >= 4 * j
                    P.add("pe", lambda h, j=j, kb=kb, off=off, sbk=sbk, diag=diag, Qa=Qa, Ka=Ka: h.matmul(
                        PS[sbk][:, off:512], Ka[0:72, kb * 128:(kb + 1) * 128],
                        Qa[0:72, j * 512 + off:(j + 1) * 512], start=True, stop=(not diag)),
                        r=[("aug", aslot)], w=[psk(sbk)])
                    if diag:
                        P.add("pe", lambda h, off=off, sbk=sbk: h.matmul(
                            PS[sbk][:, off:off + 128], ident, maskb, start=False, stop=True),
                            r=["cst"], w=[psk(sbk)])
                    P.add("act", lambda h, off=off, sbk=sbk, pts=pts: h.activation(
                        out=PT[pts][:, off:512], in_=PS[sbk][:, off:512], func=AF.Exp),
                        r=[psk(sbk)], w=[("pt", pts)])
                if s_ >= LAG:
                    u, j, kb = tasks[s_ - LAG]
                    aslot, vkind, vidx, evac = units[u]
                    sbk, pts, off = info[s_ - LAG]
                    gq = (pvctr[0] + u) * 4 + j
                    if vkind == "f":
                        ob = [3 + gq % 2]
                    else:
                        ob = [3, 4] if gq % 2 == 0 else [5, 6]
                    if kb == 0:
                        if vkind == "f":
                            P.add("pe", lambda h, ob=ob: h.matmul(PS[ob[0]][:, 0:260], zt[:, 0:128], zt[:, 0:260],
                                                                 start=True, stop=False),
                                  r=["zt"], w=[psk(ob[0])])
                        else:
                            for b_ in ob:
                                P.add("pe", lambda h, b_=b_: h.matmul(PS[b_][:, 0:258], zt[:, 0:128], zt[:, 0:258],
                                                                     start=True, stop=False),
                                      r=["zt"], w=[psk(b_)])
                    for qb in range(off // 128, 4):
                        if vkind == "f":
                            last = (kb == 4 * j + 3 and qb == 3)
                            P.add("pe", lambda h, qb=qb, kb=kb, pts=pts, ob=ob, last=last, vidx=vidx: h.matmul(
                                PS[ob[0]][:, qb * 65:(qb + 1) * 65], PT[pts][:, qb * 128:(qb + 1) * 128],
                                Vf[:, kb, vidx, :], start=False, stop=last),
                                r=[("pt", pts), "V"], w=[psk(ob[0])])
                        else:
                            last = (kb == 4 * j + qb and qb % 2 == 1)
                            b_ = ob[qb // 2]
                            q2 = qb % 2
                            P.add("pe", lambda h, q2=q2, qb=qb, kb=kb, pts=pts, b_=b_, last=last: h.matmul(
                                PS[b_][:, q2 * 129:(q2 + 1) * 129], PT[pts][:, qb * 128:(qb + 1) * 128],
                                Vd[:, kb, :], start=False, stop=last),
                                r=[("pt", pts), "V"], w=[psk(b_)])
                    if kb == 4 * j + 3:
                        evac(j, ob)
            pvctr[0] += len(units)

        def fg_part1(l):
            P.add("pool", lambda h: h.dma_start(out=wfg[:].rearrange("p k c -> p (k c)"), in_=wfg_d[l]), w=["wfg"], dma=True)
            for tg in range(4):
                tsl = slice(tg * 512, (tg + 1) * 512)
                for kc in range(8):
                    P.add("pe", lambda h, kc=kc, tsl=tsl: h.matmul(PS[7][0:8, :], wfg[:, kc, :], hT[:, kc, tsl],
                                                                   start=(kc == 0), stop=(kc == 7)),
                          r=["wfg", ("hT", tg)], w=[psk(7)])
                P.add("act", lambda h, tsl=tsl: h.activation(out=cumv[0:8, tsl], in_=PS[7][0:8, :], func=AF.Exp,
                                                             scale=-1.0, bias=negbf[:, 0:1]),
                      r=[psk(7), "negbf"], w=["cum"])
            P.add("act", lambda h: h.activation(out=cumv[0:8, :], in_=cumv[0:8, :], func=AF.Ln, scale=1.0, bias=1.0),
                  r=["cum"], w=["cum"])

        def fg_scan(l):
            P.add("dve", lambda h: h.tensor_tensor_scan(out=cumv[0:8, :], data0=ones8[:, 0:1].broadcast_to([8, S]),
                                                        data1=cumv[0:8, :], initial=0.0, op0=ALU.mult, op1=ALU.add),
                  r=["cum", "ones8"], w=["cum"])

        def ffn_gateup(l, hh, modnext=False, hook=None):
            bctr = 0
            for blk in range(11):
                if modnext and 1 <= blk <= 8:
                    mod_blocks(l + 1, [blk - 1])
                if hook and blk in hook:
                    hook[blk]()
                ws = wslot()
                wv = WS[ws][:].rearrange("p (k c) -> p k c", k=8)
                P.add("pool", lambda h, ws=ws, blk=blk: h.dma_start(
                    out=WS[ws][:].rearrange("p (a b) -> p a b", a=2),
                    in_=wgu_d[l, blk].rearrange("p (a b) -> p a b", a=2)),
                    w=[("w", ws)], dma=True)
                for ci in range(2):
                    c = blk * 2 + ci
                    for tgl in range(2):
                        tg = 2 * hh + tgl
                        tsl = slice(tg * 512, (tg + 1) * 512)
                        gb = 2 * (bctr % 3)
                        bctr += 1
                        for kc in range(8):
                            P.add("pe", lambda h, kc=kc, gb=gb, ci=ci, tsl=tsl, wv=wv: h.matmul(
                                PS[gb][:], wv[:, kc, ci * 128:(ci + 1) * 128], hT[:, kc, tsl], start=(kc == 0), stop=(kc == 7)),
                                r=[("w", ws), ("hT", tg)], w=[psk(gb)])
                        for kc in range(8):
                            P.add("pe", lambda h, kc=kc, gb=gb, ci=ci, tsl=tsl, wv=wv: h.matmul(
                                PS[gb + 1][:], wv[:, kc, 256 + ci * 128:256 + (ci + 1) * 128], hT[:, kc, tsl],
                                start=(kc == 0), stop=(kc == 7)),
                                r=[("w", ws), ("hT", tg)], w=[psk(gb + 1)])
                        sg = SG[bctr % 2]
                        sgk = ("sg", bctr % 2)
                        P.add("act", lambda h, gb=gb, sg=sg: h.activation(out=sg[:], in_=PS[gb][:], func=AF.Silu),
                              r=[psk(gb)], w=[sgk])
                        P.add("dve", lambda h, gb=gb, sg=sg, c=c, tgl=tgl: h.tensor_tensor(
                            out=actT[:, c, tgl * 512:(tgl + 1) * 512], in0=sg[:], in1=PS[gb + 1][:], op=ALU.mult),
                            r=[psk(gb + 1), sgk], w=[("actT", c)])

        def ffn_down(l, hh, extra=None):
            extra = collections.deque(extra or ())
            for oc in range(8):
                wdi = oc % 2
                wdf = WD[wdi]
                wd = wdf.rearrange("p (c n) -> p c n", c=NFC)
                P.add("pool", lambda h, wdf=wdf, oc=oc: h.dma_start(
                    out=wdf.rearrange("p (a b) -> p a b", a=2), in_=wdn_d[l, oc].rearrange("p (a b) -> p a b", a=2)),
                    w=[("wd", wdi)] + WD_OVERLAP, dma=True)
                for tgl in range(2):
                    tg = 2 * hh + tgl
                    tsl = slice(tg * 512, (tg + 1) * 512)
                    yb = 6 + (oc * 2 + tgl) % 2
                    for c in range(NFC):
                        P.add("pe", lambda h, yb=yb, c=c, wd=wd, tgl=tgl: h.matmul(
                            PS[yb][:], wd[:, c, :], actT[:, c, tgl * 512:(tgl + 1) * 512], start=(c == 0), stop=(c == NFC - 1)),
                            r=[("wd", wdi), ("actT", c)], w=[psk(yb)])
                    P.add("dve", lambda h, yb=yb, oc=oc, tsl=tsl: h.scalar_tensor_tensor(
                        out=xT[:, oc, tsl], in0=PS[yb][:], scalar=modT[l][:, 40 + oc:41 + oc], in1=xT[:, oc, tsl],
                        op0=ALU.mult, op1=ALU.add),
                        r=[psk(yb)] + mk(l, 40, 48), w=[("xT", tg), ("xTo", tg, oc)])
                    if l == depth - 1:
                        outs.append(P.add("sp", lambda h, oc=oc, tsl=tsl: h.dma_start(
                            out=outT_d[oc * 128:(oc + 1) * 128, tsl], in_=xT[:, oc, tsl]),
                            r=[("xTo", tg, oc)], dma=True))
                    if extra:
                        extra.popleft()()
            while extra:
                extra.popleft()()

        def layer(l):
            lam_init = 0.8 - 0.6 * math.exp(-0.3 * l)
            ppl = pp[:, l, :]
            P.add("dve", lambda h: h.tensor_copy(out=qkgs[:], in_=ppl[:, O_QKG:O_QKG + 4]), r=["pp"], w=["qkgs"])
            P.add("dve", lambda h: h.tensor_scalar(out=qkgs[:, 0:1], in0=ppl[:, O_QKG:O_QKG + 1], scalar1=0.125, scalar2=None, op0=ALU.mult),
                  r=["pp"], w=["qkgs"])
            P.add("dve", lambda h: h.tensor_scalar(out=qkgs[:, 2:3], in0=ppl[:, O_QKG + 2:O_QKG + 3], scalar1=0.125, scalar2=None, op0=ALU.mult),
                  r=["pp"], w=["qkgs"])
            P.add("dve", lambda h: h.tensor_scalar(out=negbf[:], in0=ppl[0:8, O_BF:O_BF + 1], scalar1=-1.0, scalar2=None, op0=ALU.mult),
                  r=["pp"], w=["negbf"])
            P.add("dve", lambda h: h.tensor_scalar(out=gnb[:], in0=ppl[:, O_GN:O_GN + 128], scalar1=(1.0 - lam_init), scalar2=None, op0=ALU.mult),
                  r=["pp"], w=["gnb"])
            for i in range(2):
                P.add("dve", lambda h, i=i: h.tensor_tensor(out=lamp[:], in0=ppl[:, O_LAM + 128 * i:O_LAM + 128 * i + 64],
                                                            in1=ppl[:, O_LAM + 128 * i + 64:O_LAM + 128 * i + 128], op=ALU.mult),
                      r=["pp"], w=["lamp"])
                P.add("dve", lambda h, i=i: h.tensor_reduce(out=lamt[:, i:i + 1], in_=lamp[:], axis=AX.X, op=ALU.add),
                      r=["lamp"], w=["lamt"])
            P.add("act", lambda h: h.activation(out=lamt[:, 2:4], in_=lamt[:, 0:2], func=AF.Exp), r=["lamt"], w=["lamt"])
            P.add("dve", lambda h: h.scalar_tensor_tensor(out=neglam[:], in0=lamt[:, 3:4], scalar=-lam_init, in1=lamt[:, 2:3],
                                                         op0=ALU.add, op1=ALU.subtract), r=["lamt"], w=["neglam"])


            P.barrier(BAR_ENGS)
            P.add("dve", lambda h: h.memset(Vf[:, :, :, 64:65], 1.0), w=["V"])
            P.add("dve", lambda h: h.memset(Vd[:, :, 128:129], 1.0), w=["V"])
            for s_ in range(2):
                P.add("pool", lambda h, s_=s_: h.dma_start(out=AUG[s_][0][68:72, :], in_=onesrow_d[0:4, :]),
                      w=[("aug", s_)] + FFN_KEYS, dma=True)
                P.add("pool", lambda h, s_=s_: h.dma_start(out=AUG[s_][1][64:68, :], in_=onesrow_d[4:8, :]),
                      w=[("aug", s_)] + FFN_KEYS, dma=True)
            fg_part1(l)

            def winp_dma(pidx):
                ws = wslot()
                P.add("pool", lambda h, ws=ws, pidx=pidx: h.dma_start(
                    out=WS[ws][:, 0:3072].rearrange("p (a b) -> p a b", a=2),
                    in_=winp_d[l, pidx].rearrange("p (a b) -> p a b", a=2)),
                    w=[("w", ws)], dma=True)
                return ws

            ws_next = winp_dma(0)
            for pidx in range(8):
                fox = pidx < 4
                pr = pidx if fox else pidx - 4
                ws = ws_next
                wv = WS[ws][:, 0:3072].rearrange("p (k c) -> p k c", k=8)
                combos = [(wi, tg) for wi in range(2) for tg in range(4)]
                if pidx == 0:
                    fg_scan(l)
                QL = 2
                for i in range(len(combos) + QL):
                    if pidx == 0 and i in (2, 4, 6, 8):
                        split_rows(8, cums_d, [i // 2 - 1])
                    if i >= QL:
                        ic = i - QL
                        wi, tg = combos[ic]
                        tsl = slice(tg * 512, (tg + 1) * 512)
                        ub = ic % 3
                        sq = SQ[ic % 2]
                        ptile, pkey = (pairq, "pairq") if wi == 0 else (pairk, "pairk")
                        gcol = (0 if fox else 2) + wi
                        tk = ic % 2
                        T = TMP[tk]
                        ssb = 5 + tk
                        P.add("pe", lambda h, sq=sq, ssb=ssb: h.matmul(PS[ssb][:], bdones, sq[:], start=True, stop=True),
                              r=[("sq", ic % 2), "cst"], w=[psk(ssb)])
                        P.add("act", lambda h, T=T, ssb=ssb: h.activation(out=T[:], in_=PS[ssb][:], func=AF.Ln, scale=1.0 / 64, bias=EPS),
                              r=[psk(ssb)], w=[("tmp", tk)])
                        P.add("act", lambda h, T=T: h.activation(out=T[:], in_=T[:], func=AF.Exp, scale=-0.5),
                              r=[("tmp", tk)], w=[("tmp", tk)])
                        P.add("dve", lambda h, ub=ub, tsl=tsl, ptile=ptile, gcol=gcol, T=T: h.scalar_tensor_tensor(
                            out=ptile[:, tsl], in0=PS[ub][:], scalar=qkgs[:, gcol:gcol + 1], in1=T[:],
                            op0=ALU.mult, op1=ALU.mult),
                            r=[psk(ub), ("tmp", tk), "qkgs"], w=[pkey])
                    if i < len(combos):
                        wi, tg = combos[i]
                        tsl = slice(tg * 512, (tg + 1) * 512)
                        ub = i % 3
                        for kc in range(8):
                            P.add("pe", lambda h, kc=kc, ub=ub, wi=wi, tsl=tsl, wv=wv: h.matmul(
                                PS[ub][:], wv[:, kc, wi * 128:(wi + 1) * 128], hT[:, kc, tsl], start=(kc == 0), stop=(kc == 7)),
                                r=[("w", ws), ("hT", tg)], w=[psk(ub)])
                        sq = SQ[i % 2]
                        P.add("act", lambda h, ub=ub, sq=sq: h.activation(out=sq[:], in_=PS[ub][:], func=AF.Square),
                              r=[psk(ub)], w=[("sq", i % 2)])
                for t4 in range(4):
                    vb = 3 + t4 % 2
                    for tt in range(4):
                        tok = (t4 * 4 + tt) * 128
                        for kc in range(8):
                            P.add("pe", lambda h, kc=kc, vb=vb, tt=tt, tok=tok, wv=wv: h.matmul(
                                PS[vb][:, tt * 128:(tt + 1) * 128], hT[:, kc, tok:tok + 128], wv[:, kc, 256:384],
                                start=(kc == 0), stop=(kc == 7)),
                                r=[("w", ws), ("hT", tok // 512)], w=[psk(vb)])
                    if fox:
                        P.add("dve", lambda h, vb=vb, t4=t4: h.tensor_copy(
                            out=Vf[:, t4 * 4:(t4 + 1) * 4, :, 0:64],
                            in_=PS[vb][:].rearrange("p (t h d) -> p t h d", t=4, h=2)),
                            r=[psk(vb)], w=["V"])
                    else:
                        P.add("dve", lambda h, vb=vb, t4=t4: h.tensor_copy(
                            out=Vd[:, t4 * 4:(t4 + 1) * 4, 0:128],
                            in_=PS[vb][:].rearrange("p (t d) -> p t d", t=4)),
                            r=[psk(vb)], w=["V"])
                for sub in range(2):
                    aslot = sub
                    Qa, Ka = AUG[aslot]
                    if fox:
                        head = 2 * pr + sub
                        bsrc = cums_d[head]
                        bkey = ("splitd", id(cums_d))
                    else:
                        head = pr
                        bsrc = alsp_d[head]
                        bkey = ("splitd", id(alsp_d))
                    P.add("sp", lambda h, Qa=Qa, sub=sub: h.dma_start(out=Qa[0:64, :], in_=pairq[sub * 64:(sub + 1) * 64, :]),
                          r=["pairq"], w=[("aug", aslot)], dma=True)
                    P.add("sp", lambda h, Ka=Ka, sub=sub: h.dma_start(out=Ka[0:64, :], in_=pairk[sub * 64:(sub + 1) * 64, :]),
                          r=["pairk"], w=[("aug", aslot)], dma=True)
                    P.add("sp", lambda h, Qa=Qa, bsrc=bsrc: h.dma_start(out=Qa[64:68, :], in_=bsrc),
                          r=[bkey], w=[("aug", aslot)], dma=True)
                    P.add("sp", lambda h, Ka=Ka, bsrc=bsrc: h.dma_start(out=Ka[68:72, :], in_=bsrc),
                          r=[bkey], w=[("aug", aslot)], dma=True)

                if pidx < 7:
                    ws_next = winp_dma(pidx + 1)
                mod_mis = [mod_dma(l, 8 + 2 * pidx), mod_dma(l, 9 + 2 * pidx)]
                if pidx in (3, 7):
                    ws2 = wslot()
                    wo = WS[ws2][:, 0:4096].rearrange("p (k c) -> p k c", k=4)
                    P.add("pool", lambda h, wo=wo, pidx=pidx: h.dma_start(
                        out=wo, in_=w_out_d[l][(0 if pidx == 3 else 1) * 512:((0 if pidx == 3 else 1) + 1) * 512, :].rearrange("(k p) c -> p k c", p=128)),
                        w=[("w", ws2)], dma=True)

                units = []
                for sub in range(2):
                    if fox:
                        head = 2 * pr + sub

                        def evac(j, ob, head=head):
                            O = PS[ob[0]][:, 0:260].rearrange("p (q d) -> p q d", q=4)
                            P.add("dve", lambda h, O=O: h.reciprocal(out=rc[:, 0:4], in_=O[:, :, 64]),
                                  r=[psk(ob[0])], w=["rc"])
                            for qb in range(4):
                                P.add("dve", lambda h, O=O, qb=qb, j=j: h.tensor_scalar(
                                    out=mixed[:, 4 * j + qb, head * 64:(head + 1) * 64], in0=O[:, qb, 0:64],
                                    scalar1=rc[:, qb:qb + 1], scalar2=None, op0=ALU.mult),
                                    r=[psk(ob[0]), "rc"], w=[("mixed", 4 * j + qb)])

                        units.append((sub, "f", sub, evac))
                    else:
                        head = pr
                        if sub == 0:
                            def evac(j, ob, head=head):
                                for qb in range(4):
                                    O = PS[ob[qb // 2]][:, 0:258].rearrange("p (q d) -> p q d", q=2)
                                    q2 = qb % 2
                                    P.add("dve", lambda h, O=O, q2=q2, qb=qb: h.reciprocal(out=rc[:, qb:qb + 1], in_=O[:, q2, 128:129]),
                                          r=[psk(ob[qb // 2])], w=["rc"])
                                    P.add("dve", lambda h, O=O, q2=q2, qb=qb, j=j: h.tensor_scalar(
                                        out=t1[:, 4 * j + qb, :], in0=O[:, q2, 0:128], scalar1=rc[:, qb:qb + 1], scalar2=None,
                                        op0=ALU.mult),
                                        r=[psk(ob[qb // 2]), "rc"], w=["cum"])
                        else:
                            def evac(j, ob, head=head):
                                for qb in range(4):
                                    O = PS[ob[qb // 2]][:, 0:258].rearrange("p (q d) -> p q d", q=2)
                                    q2 = qb % 2
                                    P.add("dve", lambda h, O=O, q2=q2, qb=qb: h.reciprocal(out=rc[:, qb:qb + 1], in_=O[:, q2, 128:129]),
                                          r=[psk(ob[qb // 2])], w=["rc"])
                                    P.add("dve", lambda h, qb=qb: h.tensor_scalar(
                                        out=rc[:, 4 + qb:5 + qb], in0=rc[:, qb:qb + 1], scalar1=neglam[:, 0:1], scalar2=None, op0=ALU.mult),
                                        r=["rc", "neglam"], w=["rc"])
                                    P.add("dve", lambda h, O=O, q2=q2, qb=qb, j=j: h.scalar_tensor_tensor(
                                        out=t1[:, 4 * j + qb, :], in0=O[:, q2, 0:128], scalar=rc[:, 4 + qb:5 + qb],
                                        in1=t1[:, 4 * j + qb, :], op0=ALU.mult, op1=ALU.add),
                                        r=[psk(ob[qb // 2]), "rc"], w=["cum"])
                                    P.add("dve", lambda h, qb=qb, j=j: h.scalar_tensor_tensor(
                                        out=junk[:], in0=t1[:, 4 * j + qb, :], scalar=1.0, in1=t1[:, 4 * j + qb, :],
                                        op0=ALU.mult, op1=ALU.mult, accum_out=ssd[:, qb:qb + 1]),
                                        r=["cum"], w=["ssd", "junk"])
                                P.add("dve", lambda h: h.tensor_scalar(out=ssd[:, 4:8], in0=ssd[:, 0:4], scalar1=1.0 / 128, scalar2=EPS,
                                                                       op0=ALU.mult, op1=ALU.add),
                                      r=["ssd"], w=["ssd"])
                                P.add("pool", lambda h: h.tensor_tensor(out=ssd[:, 4:8], in0=ssd[:, 4:8], in1=mhalf[:, 0:4], op=ALU.pow),
                                      r=["ssd", "mhalf"], w=["ssd"])
                                for qb in range(4):
                                    P.add("dve", lambda h, qb=qb, j=j: h.scalar_tensor_tensor(
                                        out=mixed[:, 4 * j + qb, head * 128:(head + 1) * 128], in0=t1[:, 4 * j + qb, :],
                                        scalar=ssd[:, 4 + qb:5 + qb], in1=gnb[:], op0=ALU.mult, op1=ALU.mult),
                                        r=["cum", "ssd", "gnb"], w=[("mixed", 4 * j + qb)])

                        units.append((sub, "d", 0, evac))

                attention_units(units)

                mod_blocks(l, [8 + 2 * pidx, 9 + 2 * pidx], mod_mis)

                if pidx in (3, 7):
                    hf = 0 if pidx == 3 else 1
                    xnb = [XN[0][:].bitcast(BF16), XN[1][:].bitcast(BF16)]
                    MT = [[PT[0], PT[1], PT[2], PT[3]],
                          [xnb[0][:, 0:512], xnb[0][:, 512:1024], xnb[1][:, 0:512], xnb[1][:, 512:1024]]]
                    MTK = [[("pt", 0), ("pt", 1), ("pt", 2), ("pt", 3)], [("xn", 0), ("xn", 0), ("xn", 1), ("xn", 1)]]
                    TB = [(0, 1), (5, 6)]

                    def op_transposes(tg):
                        par = tg % 2
                        for fc in range(4):
                            tb = TB[par][fc % 2]
                            tpv = PS[tb][:].bitcast(BF16)
                            for tt in range(4):
                                P.add("pe", lambda h, tpv=tpv, tt=tt, fc=fc, tg=tg: h.transpose(
                                    tpv[:, tt * 128:(tt + 1) * 128], mixed[:, 4 * tg + tt, fc * 128:(fc + 1) * 128], ident),
                                    r=[("mixed", 4 * tg + tt), "cst"], w=[psk(tb)])
                            mt = MT[par][fc]
                            if fc % 2 == 0:
                                P.add("act", lambda h, tpv=tpv, mt=mt: h.activation(out=mt, in_=tpv[:, 0:512], func=AF.Copy),
                                      r=[psk(tb)], w=[MTK[par][fc]])
                            else:
                                P.add("dve", lambda h, tpv=tpv, mt=mt: h.tensor_copy(out=mt, in_=tpv[:, 0:512]),
                                      r=[psk(tb)], w=[MTK[par][fc]])

                    op_transposes(0)
                    for tg in range(4):
                        tsl = slice(tg * 512, (tg + 1) * 512)
                        par = tg % 2
                        if tg + 1 < 4:
                            op_transposes(tg + 1)
                        for oc in range(8):
                            yb = 2 + oc % 3
                            for fc in range(4):
                                P.add("pe", lambda h, yb=yb, fc=fc, oc=oc, wo=wo, par=par: h.matmul(
                                    PS[yb][:], wo[:, fc, oc * 128:(oc + 1) * 128], MT[par][fc], start=(fc == 0), stop=(fc == 3)),
                                    r=[("w", ws2), MTK[par][fc]], w=[psk(yb)])
                            P.add("dve", lambda h, yb=yb, oc=oc, tsl=tsl: h.scalar_tensor_tensor(
                                out=xT[:, oc, tsl], in0=PS[yb][:], scalar=modT[l][:, 16 + oc:17 + oc], in1=xT[:, oc, tsl],
                                op0=ALU.mult, op1=ALU.add),
                                r=[psk(yb)] + mk(l, 16, 24), w=[("xT", tg)])
                        if hf == 1 and tg < 2:
                            rms_p1(tg)

            mod_a(l, 2)
            P.barrier(BAR_ENGS)
            nxt = l + 1 < depth
            rms_p2(l, 0, a2[l], ("a2", l), 24)
            rms_p2(l, 1, a2[l], ("a2", l), 24)
            ffn_gateup(l, 0, modnext=nxt, hook={6: lambda: rms_p1(2), 8: lambda: rms_p1(3)})
            if nxt:
                mod_a(l + 1, 1)
            ffn_down(l, 0, extra=rms_p2_thunks(l, 2, a2[l], ("a2", l), 24) + rms_p2_thunks(l, 3, a2[l], ("a2", l), 24))
            ffn_gateup(l, 1, hook=({6: lambda: rms_p1(0), 8: lambda: rms_p1(1)} if nxt else None))
            ffn_down(l, 1, extra=(rms_p2_thunks(l + 1, 0, a1[l + 1], ("a1", l + 1), 0)
                                  + rms_p2_thunks(l + 1, 1, a1[l + 1], ("a1", l + 1), 0)) if nxt else None)
            if nxt:
                rms_seq(l + 1, (2, 3), a1[l + 1], ("a1", l + 1), 0)

        outs = []
        prologue()
        for l in range(depth):
            layer(l)

        ops = P.emit(sems, dsems)
        for oi in outs:
            op = ops[oi]
            nc.sync.wait_ge(op["dsem"], op["dval"])
    return nc


_NC_CACHE = {}


def _host_inputs(inp):
    f = lambda a: np.ascontiguousarray(np.asarray(a, dtype=np.float32))
    x = f(inp["x"]); c = f(inp["c"])
    B = x.shape[0]
    pp = np.zeros((DEPTH, 128, NP_), np.float32)
    for l in range(DEPTH):
        pp[l, :, O_LN1:O_LN1 + 8] = f(inp["ln1_g"])[l].reshape(8, 128).T
        pp[l, :, O_LN2:O_LN2 + 8] = f(inp["ln2_g"])[l].reshape(8, 128).T
        pp[l, :, O_BADA:O_BADA + 48] = f(inp["b_ada"])[l].reshape(48, 128).T
        pp[l, :, O_QKG + 0] = np.tile(f(inp["fox_qk_g"])[l, 0], 2)
        pp[l, :, O_QKG + 1] = np.tile(f(inp["fox_qk_g"])[l, 1], 2)
        pp[l, :, O_QKG + 2] = np.tile(f(inp["diff_qk_g"])[l, 0], 2)
        pp[l, :, O_QKG + 3] = np.tile(f(inp["diff_qk_g"])[l, 1], 2)
        pp[l, :, O_LAM:O_LAM + 256] = np.broadcast_to(f(inp["diff_lam"])[l].reshape(1, 256), (128, 256))
        pp[l, :, O_GN:O_GN + 128] = np.broadcast_to(f(inp["diff_norm_g"])[l].reshape(1, 128), (128, 128))
        pp[l, 0:8, O_BF] = f(inp["b_f"])[l]
    cst = np.zeros((128, 512), np.float32)
    cst[:, 0:128] = np.eye(128, dtype=np.float32)
    kk = np.arange(128)[:, None]; qq = np.arange(128)[None, :]
    cst[:, 128:256] = np.where(kk <= qq, 0.0, -30000.0)
    cst[0:64, 256:320] = 1.0
    cst[64:128, 320:384] = 1.0
    cst[:, 384:512] = 1.0
    slopes = np.array([2.0 ** (-8.0 * (h + 1) / 4) for h in range(4)], np.float32)
    import ml_dtypes
    al = (slopes[:, None] * np.arange(S, dtype=np.float32)[None, :]).astype(np.float32)
    parts = []
    rem = al.copy()
    for _ in range(4):
        p_ = rem.astype(ml_dtypes.bfloat16).astype(np.float32)
        parts.append(p_)
        rem = (rem - p_).astype(np.float32)
    alibi = np.ascontiguousarray(np.stack(parts, 1))
    onesrow = np.concatenate([np.ones((4, S), np.float32), -np.ones((4, S), np.float32)], 0)
    w_ada = f(inp["w_ada"]); w_in = f(inp["w_in"]); w_gate = f(inp["w_gate"]); w_up = f(inp["w_up"]); w_down = f(inp["w_down"])
    wada = np.ascontiguousarray(w_ada.reshape(DEPTH, 8, 128, 24, 256).transpose(0, 3, 2, 1, 4)).reshape(DEPTH, 24, 128, 2048)
    winp = np.empty((DEPTH, 8, 128, 8, 384), np.float32)
    for pidx in range(8):
        base = 0 if pidx < 4 else 1544
        pr = pidx % 4
        for wi in range(3):
            c0 = base + wi * 512 + pr * 128
            winp[:, pidx, :, :, wi * 128:(wi + 1) * 128] = w_in[:, :, c0:c0 + 128].reshape(DEPTH, 8, 128, 128).transpose(0, 2, 1, 3)
    winp = winp.reshape(DEPTH, 8, 128, 3072)
    wfg = np.ascontiguousarray(w_in[:, :, 1536:1544].reshape(DEPTH, 8, 128, 8).transpose(0, 2, 1, 3)).reshape(DEPTH, 128, 64)
    wgu = np.empty((DEPTH, 11, 128, 8, 512), np.float32)
    wgu[..., 0:256] = w_gate.reshape(DEPTH, 8, 128, 11, 256).transpose(0, 3, 2, 1, 4)
    wgu[..., 256:512] = w_up.reshape(DEPTH, 8, 128, 11, 256).transpose(0, 3, 2, 1, 4)
    wgu = wgu.reshape(DEPTH, 11, 128, 4096)
    wdn = np.ascontiguousarray(w_down.reshape(DEPTH, NFC, 128, 8, 128).transpose(0, 3, 2, 1, 4)).reshape(DEPTH, 8, 128, DFF)
    shared = dict(pp=pp, cst=cst, alibi=alibi, onesrow=onesrow, wada=wada, winp=winp, wfg=wfg,
                  w_out=f(inp["w_out"]), wgu=wgu, wdn=wdn)
    maps = []
    for b in range(B):
        m = dict(shared)
        m["xT"] = np.ascontiguousarray(x[b].T)
        m["cT"] = np.ascontiguousarray(c[b].reshape(8, 128).T)
        maps.append(m)
    return maps


def kernel(**inputs):
    maps = _host_inputs(inputs)
    if "nc" not in _NC_CACHE:
        _NC_CACHE["nc"] = build()
    nc = _NC_CACHE["nc"]
    res = run_bass_kernel_spmd(nc, maps, core_ids=list(range(len(maps))))
    out = np.stack([np.ascontiguousarray(np.asarray(r["outT"], dtype=np.float32).T) for r in res.results], 0)
    return out
```

```python
import math
import collections
import contextlib
import numpy as np
import concourse.bass as bass
import concourse.mybir as mybir
from concourse.bass_utils import run_bass_kernel_spmd

F32 = mybir.dt.float32
BF16 = mybir.dt.bfloat16
AF = mybir.ActivationFunctionType
ALU = mybir.AluOpType
AX = mybir.AxisListType

D = 1024
S = 2048
DEPTH = 2
DFF = 2816
NFC = DFF // 128
INW = 3080
EPS = 1e-6
NP_ = 8 + 8 + 48 + 4 + 256 + 128 + 1
O_LN1, O_LN2, O_BADA, O_QKG, O_LAM, O_GN, O_BF = 0, 8, 16, 64, 68, 324, 452

ENGS = ("pe", "act", "dve", "pool", "sp")


class Prog:
    def __init__(self, nc):
        self.nc = nc
        self.ops = []
        self.lastw = {}
        self.rd_c = {}
        self.rd_d = {}
        self.ecount = {e: 0 for e in ENGS}
        self.pending = {e: set() for e in ENGS}

    def barrier(self, engines=ENGS):
        lastc = {}
        lastd = {"sp": [], "pool": []}
        for op in self.ops:
            if op["dma"]:
                lastd[op["eng"]].append(op["idx"])
            else:
                lastc[op["eng"]] = op["idx"]
        deps = set(lastc.values())
        for q in lastd:
            deps.update(lastd[q][-8:])
        for e in engines:
            self.pending[e] = set(deps)

    def add(self, eng, fn, r=(), w=(), dma=False):
        idx = len(self.ops)
        deps = set()
        if self.pending[eng]:
            deps |= self.pending[eng]
            self.pending[eng] = set()
        for k in r:
            if k in self.lastw:
                deps.add(self.lastw[k])
        for k in w:
            if k in self.lastw:
                deps.add(self.lastw[k])
            for v in self.rd_c.get(k, {}).values():
                deps.add(v)
            for v in self.rd_d.get(k, ()):
                deps.add(v)
        for k in w:
            self.lastw[k] = idx
            self.rd_c[k] = {}
            self.rd_d[k] = []
        ws = set(w)
        for k in r:
            if k in ws:
                continue
            if dma:
                self.rd_d.setdefault(k, []).append(idx)
            else:
                self.rd_c.setdefault(k, {})[eng] = idx
        deps.discard(idx)
        self.ops.append(dict(eng=eng, fn=fn, dma=dma, idx=idx, eidx=self.ecount[eng], deps=deps, ms=False))
        self.ecount[eng] += 1
        return idx

    def emit(self, sems, dsems):
        nc = self.nc
        H = {"pe": nc.tensor, "act": nc.scalar, "dve": nc.vector, "pool": nc.gpsimd, "sp": nc.sync}
        ops = self.ops
        dcnt = {q: 0 for q in dsems}
        dtot = {}
        for op in ops:
            if op["dma"]:
                q = op["eng"]
                pool = dsems[q]
                si = dcnt[q] % len(pool)
                dcnt[q] += 1
                prev = dtot.get((q, si), 0)
                op["dsem"] = pool[si]
                op["dkey"] = (q, si)
                op["dprev"] = prev
                op["dval"] = prev + 16
                dtot[(q, si)] = prev + 16
        for op in ops:
            for di in op["deps"]:
                y = ops[di]
                if y["dma"]:
                    continue
                if y["eng"] != op["eng"]:
                    y["ms"] = True
                elif op["eng"] != "pe" and (op["eidx"] - y["eidx"] <= 3):
                    y["ms"] = True
        mcount = {e: 0 for e in ENGS}
        for op in ops:
            if op["ms"]:
                mcount[op["eng"]] += 1
                op["msval"] = mcount[op["eng"]]
        waited = {e: {} for e in ENGS}
        nwait = 0
        for op in ops:
            e = op["eng"]
            h = H[e]
            need = {}
            for di in op["deps"]:
                y = ops[di]
                if y["dma"]:
                    key = ("d",) + y["dkey"]
                    need[key] = max(need.get(key, 0), y["dval"])
                else:
                    if y["eng"] == e and (e == "pe" or op["eidx"] - y["eidx"] > 3):
                        continue
                    key = ("c", y["eng"])
                    need[key] = max(need.get(key, 0), y["msval"])
            if op["dma"] and op["dprev"] > 0:
                key = ("d",) + op["dkey"]
                need[key] = max(need.get(key, 0), op["dprev"])
            for key, val in need.items():
                if waited[e].get(key, 0) >= val:
                    continue
                waited[e][key] = val
                sem = sems[key[1]] if key[0] == "c" else dsems[key[1]][key[2]]
                h.wait_ge(sem, val)
                nwait += 1
            ins = op["fn"](h)
            if op["dma"]:
                ins.then_inc(op["dsem"], 16)
            elif op["ms"]:
                ins.then_inc(sems[e], 1)
        self.nwait = nwait
        return ops


def build(depth=DEPTH):
    nc = bass.Bass("TRN2", target_bir_lowering=False)

    def din(name, shape):
        return nc.dram_tensor(name, shape, F32, kind="ExternalInput").ap()

    xT_d = din("xT", [D, S])
    cT_d = din("cT", [128, 8])
    pp_d = din("pp", [DEPTH, 128, NP_])
    cst_d = din("cst", [128, 512])
    alibi_d = din("alibi", [4, 4, S])
    onesrow_d = din("onesrow", [8, S])
    wada_d = din("wada", [DEPTH, 24, 128, 2048])
    winp_d = din("winp", [DEPTH, 8, 128, 3072])
    wfg_d = din("wfg", [DEPTH, 128, 64])
    w_out_d = din("w_out", [DEPTH, D, D])
    wgu_d = din("wgu", [DEPTH, 11, 128, 4096])
    wdn_d = din("wdn", [DEPTH, 8, 128, DFF])
    outT_d = nc.dram_tensor("outT", [D, S], F32, kind="ExternalOutput").ap()
    cums_d = nc.dram_tensor("cums", [8, 4, S], BF16, kind="Internal").ap()
    alsp_d = nc.dram_tensor("alsp", [4, 4, S], BF16, kind="Internal").ap()

    es = contextlib.ExitStack()
    with es:
        def sb(name, shape, dt):
            return es.enter_context(nc.sbuf_tensor(name, shape, dt))

        xT = sb("xT_sb", [128, 8, S], F32)
        hT = sb("hT", [128, 8, S], BF16)
        WS = [sb(f"ws{i}", [128, 4096], BF16) for i in range(3)]
        MW = [sb(f"mw{i}", [128, 2048], BF16) for i in range(2)]
        RCOLS = 30848
        R = sb("R", [128, RCOLS], BF16)
        o = 0
        AUG = []
        for s_ in range(2):
            q_ = R[:, o:o + 2048]; o += 2048
            k_ = R[:, o:o + 2048]; o += 2048
            AUG.append((q_, k_))
        pairq = R[:, o:o + 2048]; o += 2048
        pairk = R[:, o:o + 2048]; o += 2048
        Vf = R[:, o:o + 16 * 130].rearrange("p (t h d) -> p t h d", t=16, h=2); o += 2112
        Vd = R[:, o:o + 16 * 129].rearrange("p (t d) -> p t d", t=16); o += 2112
        PTall = R[:, o:o + 2048]; o += 2048
        PT = [PTall[:, i * 512:(i + 1) * 512] for i in range(4)]
        mixed = R[:, o:o + 8192].rearrange("p (t c) -> p t c", t=16); o += 8192
        cumt1_b = R[:, o:o + 4096]; o += 4096
        cumv = cumt1_b.bitcast(F32)
        t1 = cumv.rearrange("p (t d) -> p t d", t=16)
        att_end = o
        o = 0
        actT = R[:, o:o + NFC * 1024].rearrange("p (c t) -> p c t", c=NFC); o += NFC * 1024
        WD = []
        for i in range(2):
            WD.append(R[:, o:o + NFC * 128]); o += NFC * 128
        SG = []
        for i in range(2):
            SG.append(R[:, o:o + 1024].bitcast(F32)); o += 1024
        assert max(att_end, o) <= RCOLS, (att_end, o)
        WD_OVERLAP = [("mixed", t) for t in range(16)] + ["cum"]
        FFN_KEYS = [("actT", c) for c in range(NFC)] + [("wd", 0), ("wd", 1), ("sg", 0), ("sg", 1)]

        tmpA = sb("tmpA", [128, 512], F32)
        tmpB = sb("tmpB", [128, 512], F32)
        TMP = [tmpA, tmpB]
        XN = [sb(f"xn{i}", [128, 512], F32) for i in range(2)]
        SQ = [sb(f"sq{i}", [128, 512], BF16) for i in range(2)]
        cst = sb("cst_bf", [128, 512], BF16)
        ident = cst[:, 0:128]
        maskb = cst[:, 128:256]
        bdones = cst[:, 256:384]
        ones128 = cst[:, 384:512]
        zt = sb("zt", [128, 512], BF16)
        c_sb = sb("c_sb", [128, 8], F32)
        condT = sb("condT", [128, 8], BF16)
        pp = sb("pp_sb", [128, DEPTH, NP_], F32)
        modT = [sb(f"modT{l}", [128, 48], F32) for l in range(DEPTH)]
        a1 = [sb(f"a1_{l}", [128, 8], F32) for l in range(DEPTH)]
        a2 = [sb(f"a2_{l}", [128, 8], F32) for l in range(DEPTH)]
        qkgs = sb("qkgs", [128, 4], F32)
        negbf = sb("negbf", [8, 1], F32)
        ones8 = sb("ones8", [8, 1], F32)
        neglam = sb("neglam", [128, 1], F32)
        lamt = sb("lamt", [128, 4], F32)
        lamp = sb("lamp", [128, 64], F32)
        gnb = sb("gnb", [128, 128], F32)
        rc = sb("rc", [128, 8], F32)
        ssd = sb("ssd", [128, 8], F32)
        junk = sb("junk", [128, 128], BF16)
        mhalf = sb("mhalf", [128, 4], F32)
        wfg = sb("wfg_sb", [128, 8, 8], BF16)

        PS = [es.enter_context(nc.psum_tensor(f"ps{i}", [128, 512], F32)) for i in range(8)]

        sems = {e: es.enter_context(nc.semaphore(f"sem_{e}")) for e in ENGS}
        dsems = {
            "sp": [es.enter_context(nc.semaphore(f"dsp{i}")) for i in range(8)],
            "pool": [es.enter_context(nc.semaphore(f"dpl{i}")) for i in range(8)],
        }

        P = Prog(nc)
        wctr = [0]
        mwctr = [0]
        BAR_ENGS = ("pe", "act", "dve", "sp")

        def wslot():
            i = wctr[0] % 3
            wctr[0] += 1
            return i

        def psk(b):
            return ("ps", b)

        def mk(l, j0, j1):
            return [("modT", l, b) for b in range(j0 // 2, (j1 + 1) // 2)]

        P.add("sp", lambda h: h.dma_start(out=c_sb[:], in_=cT_d[:]), w=["c_sb"], dma=True)
        P.add("sp", lambda h: h.dma_start(out=pp[:], in_=pp_d.rearrange("l p n -> p l n")), w=["pp"], dma=True)
        def x_load(tg, q):
            P.add(q, lambda h, tg=tg: h.dma_start(
                out=xT[:, :, tg * 512:(tg + 1) * 512],
                in_=xT_d[:, tg * 512:(tg + 1) * 512].rearrange("(k p) t -> p k t", p=128)),
                w=[("xT", tg)], dma=True)

        x_load(0, "sp")
        P.add("pool", lambda h: h.dma_start(out=cst[:], in_=cst_d[:]), w=["cst"], dma=True)
        P.add("dve", lambda h: h.memset(zt[:], 0.0), w=["zt"])
        P.add("dve", lambda h: h.memset(ones8[:], 1.0), w=["ones8"])
        P.add("dve", lambda h: h.memset(mhalf[:], -0.5), w=["mhalf"])
        P.add("act", lambda h: h.activation(out=condT[:], in_=c_sb[:], func=AF.Silu), r=["c_sb"], w=["condT"])

        ptkeys = [("pt", i) for i in range(4)]

        def split_rows(nrows, dst_d, chunks=range(4)):
            spst = PTall[0:nrows, 0:2048].rearrange("p (a b) -> p a b", a=4)
            for c in chunks:
                xs = cumv[0:nrows, c * 512:(c + 1) * 512]
                for a in range(4):
                    P.add("dve", lambda h, xs=xs, a=a: h.tensor_copy(out=spst[:, a, :], in_=xs),
                          r=["cum"], w=ptkeys)
                    if a < 3:
                        P.add("dve", lambda h, xs=xs, a=a: h.tensor_tensor(out=xs, in0=xs, in1=spst[:, a, :], op=ALU.subtract),
                              r=ptkeys, w=["cum"])
                P.add("sp", lambda h, c=c: h.dma_start(out=dst_d[:, :, c * 512:(c + 1) * 512], in_=spst),
                      r=ptkeys, w=[("splitd", id(dst_d))], dma=True)

        def mod_dma(l, blk):
            mi = mwctr[0] % 2
            mwctr[0] += 1
            mw = MW[mi]
            P.add("pool", lambda h, mw=mw, blk=blk: h.dma_start(out=mw[:], in_=wada_d[l, blk]),
                  w=[("mw", mi)], dma=True)
            return mi

        def mod_blocks(l, blks, mis=None):
            for bi, blk in enumerate(blks):
                mi = mis[bi] if mis is not None else mod_dma(l, blk)
                mod_compute(l, blk, MW[mi][:], ("mw", mi))

        def mod_compute(l, blk, mwap, mwkey):
            if True:
                mi = mwkey
                wv = mwap.rearrange("p (k c) -> p k c", k=8)
                for cc in range(2):
                    for kc in range(8):
                        P.add("pe", lambda h, wv=wv, cc=cc, kc=kc: h.matmul(
                            PS[7][:, cc:cc + 1], wv[:, kc, cc * 128:(cc + 1) * 128], condT[:, kc:kc + 1],
                            start=(kc == 0), stop=(kc == 7)),
                            r=[mwkey, "condT"], w=[psk(7)])
                P.add("dve", lambda h, blk=blk: h.tensor_tensor(
                    out=modT[l][:, 2 * blk:2 * blk + 2], in0=PS[7][:, 0:2],
                    in1=pp[:, l, O_BADA + 2 * blk:O_BADA + 2 * blk + 2], op=ALU.add),
                    r=[psk(7), "pp"], w=[("modT", l, blk)])

        def mod_a(l, which):
            if which == 1:
                P.add("dve", lambda h: h.scalar_tensor_tensor(out=a1[l][:], in0=modT[l][:, 8:16], scalar=1.0,
                                                             in1=pp[:, l, O_LN1:O_LN1 + 8], op0=ALU.add, op1=ALU.mult),
                      r=mk(l, 8, 16) + ["pp"], w=[("a1", l)])
            else:
                P.add("dve", lambda h: h.scalar_tensor_tensor(out=a2[l][:], in0=modT[l][:, 32:40], scalar=1.0,
                                                             in1=pp[:, l, O_LN2:O_LN2 + 8], op0=ALU.add, op1=ALU.mult),
                      r=mk(l, 32, 40) + ["pp"], w=[("a2", l)])

        def prologue():
            x_load(1, "sp")
            bufs = [(MW[0][:], ("mw", 0)), (MW[1][:], ("mw", 1))]
            for i_ in range(3):
                bufs.append((WS[i_][:, 0:2048], ("w", i_)))
                bufs.append((WS[i_][:, 2048:4096], ("w", i_)))
            for blk in range(8):
                P.add("pool", lambda h, blk=blk: h.dma_start(out=bufs[blk][0], in_=wada_d[0, blk]),
                      w=[bufs[blk][1]], dma=True)
                if blk == 3:
                    x_load(2, "pool")
            x_load(3, "pool")
            P.add("pool", lambda h: h.dma_start(out=alsp_d, in_=alibi_d), w=[("splitd", id(alsp_d))], dma=True)
            rms_p1(0)
            rms_p1(1)
            for blk in range(8):
                mod_compute(0, blk, bufs[blk][0], bufs[blk][1])
            mod_a(0, 1)
            rms_p2(0, 0, a1[0], ("a1", 0), 0)
            rms_p1(2)
            rms_p2(0, 1, a1[0], ("a1", 0), 0)
            rms_p1(3)
            rms_p2(0, 2, a1[0], ("a1", 0), 0)
            rms_p2(0, 3, a1[0], ("a1", 0), 0)

        def rms_p1(tg):
            tsl = slice(tg * 512, (tg + 1) * 512)
            tk = tg % 2
            T = TMP[tk]
            for kc in range(8):
                wk = [("hTs", tg, kc)] + ([("hT", tg), ("hTg", tg)] if kc == 0 else [])
                rk = [("xT", tg)] + ([] if kc == 0 else [("hTg", tg)])
                if kc not in (1, 5):
                    P.add("act", lambda h, kc=kc: h.activation(out=hT[:, kc, tsl], in_=xT[:, kc, tsl], func=AF.Square),
                          r=rk, w=wk)
                else:
                    P.add("dve", lambda h, kc=kc: h.tensor_tensor(out=hT[:, kc, tsl], in0=xT[:, kc, tsl], in1=xT[:, kc, tsl], op=ALU.mult),
                          r=rk, w=wk)
            for kc in range(8):
                P.add("pe", lambda h, kc=kc: h.matmul(PS[7][:], ones128, hT[:, kc, tsl], start=(kc == 0), stop=(kc == 7)),
                      r=[("hTs", tg, kc), "cst"], w=[psk(7)])
            P.add("act", lambda h: h.activation(out=T[:], in_=PS[7][:], func=AF.Ln, scale=1.0 / D, bias=EPS),
                  r=[psk(7)], w=[("tmp", tk)])
            P.add("act", lambda h: h.activation(out=T[:], in_=T[:], func=AF.Exp, scale=-0.5),
                  r=[("tmp", tk)], w=[("tmp", tk)])

        def rms_p2_thunks(l, tg, a_t, akey, sh_col0):
            tsl = slice(tg * 512, (tg + 1) * 512)
            tk = tg % 2
            T = TMP[tk]
            shkeys = mk(l, sh_col0, sh_col0 + 8)
            th = []
            for kc in range(8):
                def one(kc=kc):
                    xn = XN[kc % 2]
                    P.add("dve", lambda h: h.scalar_tensor_tensor(
                        out=xn[:], in0=xT[:, kc, tsl], scalar=a_t[:, kc:kc + 1], in1=T[:], op0=ALU.mult, op1=ALU.mult),
                        r=[("xT", tg), ("tmp", tk), akey], w=[("xn", kc % 2)])
                    P.add("act", lambda h: h.activation(
                        out=hT[:, kc, tsl], in_=xn[:], func=AF.Identity, bias=modT[l][:, sh_col0 + kc:sh_col0 + kc + 1]),
                        r=[("xn", kc % 2)] + shkeys, w=[("hT", tg), ("hTs", tg, kc)])
                th.append(one)
            return th

        def rms_p2(l, tg, a_t, akey, sh_col0):
            for t_ in rms_p2_thunks(l, tg, a_t, akey, sh_col0):
                t_()

        def rms_seq(l, tgs, a_t, akey, sh_col0):
            tgs = list(tgs)
            rms_p1(tgs[0])
            for i_, tg in enumerate(tgs):
                if i_ + 1 < len(tgs):
                    rms_p1(tgs[i_ + 1])
                rms_p2(l, tg, a_t, akey, sh_col0)

        pvctr = [0]

        def attention_units(units):
            tasks = [(u, j, kb) for u in range(len(units)) for j in range(4) for kb in range(4 * j + 4)]
            LAG = 3
            SB = (0, 1, 2, 7)
            n = len(tasks)
            info = {}
            for s_ in range(n + LAG):
                if s_ < n:
                    u, j, kb = tasks[s_]
                    aslot = units[u][0]
                    Qa, Ka = AUG[aslot]
                    off = max(0, kb - 4 * j) * 128
                    sbk = SB[s_ % 4]
                    pts = s_ % 4
                    info[s_] = (sbk, pts, off)
                    diag = kb >= 4 * j
                    P.add("pe", lambda h, j=j, kb=kb, off=off, sbk=sbk, diag=diag, Qa=Qa, Ka=Ka: h.matmul(
                        PS[sbk][:, off:512], Ka[0:72, kb * 128:(kb + 1) * 128],
                        Qa[0:72, j * 512 + off:(j + 1) * 512], start=True, stop=(not diag)),
                        r=[("aug", aslot)], w=[psk(sbk)])
                    if diag:
                        P.add("pe", lambda h, off=off, sbk=sbk: h.matmul(
                            PS[sbk][:, off:off + 128], ident, maskb, start=False, stop=True),
                            r=["cst"], w=[psk(sbk)])
                    P.add("act", lambda h, off=off, sbk=sbk, pts=pts: h.activation(
                        out=PT[pts][:, off:512], in_=PS[sbk][:, off:512], func=AF.Exp),
                        r=[psk(sbk)], w=[("pt", pts)])
                if s_ >= LAG:
                    u, j, kb = tasks[s_ - LAG]
                    aslot, vkind, vidx, evac = units[u]
                    sbk, pts, off = info[s_ - LAG]
                    gq = (pvctr[0] + u) * 4 + j
                    if vkind == "f":
                        ob = [3 + gq % 2]
                    else:
                        ob = [3, 4] if gq % 2 == 0 else [5, 6]
                    if kb == 0:
                        if vkind == "f":
                            P.add("pe", lambda h, ob=ob: h.matmul(PS[ob[0]][:, 0:260], zt[:, 0:128], zt[:, 0:260],
                                                                 start=True, stop=False),
                                  r=["zt"], w=[psk(ob[0])])
                        else:
                            for b_ in ob:
                                P.add("pe", lambda h, b_=b_: h.matmul(PS[b_][:, 0:258], zt[:, 0:128], zt[:, 0:258],
                                                                     start=True, stop=False),
                                      r=["zt"], w=[psk(b_)])
                    for qb in range(off // 128, 4):
                        if vkind == "f":
                            last = (kb == 4 * j + 3 and qb == 3)
                            P.add("pe", lambda h, qb=qb, kb=kb, pts=pts, ob=ob, last=last, vidx=vidx: h.matmul(
                                PS[ob[0]][:, qb * 65:(qb + 1) * 65], PT[pts][:, qb * 128:(qb + 1) * 128],
                                Vf[:, kb, vidx, :], start=False, stop=last),
                                r=[("pt", pts), "V"], w=[psk(ob[0])])
                        else:
                            last = (kb == 4 * j + qb and qb % 2 == 1)
                            b_ = ob[qb // 2]
                            q2 = qb % 2
                            P.add("pe", lambda h, q2=q2, qb=qb, kb=kb, pts=pts, b_=b_, last=last: h.matmul(
                                PS[b_][:, q2 * 129:(q2 + 1) * 129], PT[pts][:, qb * 128:(qb + 1) * 128],
                                Vd[:, kb, :], start=False, stop=last),
                                r=[("pt", pts), "V"], w=[psk(b_)])
                    if kb == 4 * j + 3:
                        evac(j, ob)
            pvctr[0] += len(units)

        def fg_part1(l):
            P.add("pool", lambda h: h.dma_start(out=wfg[:].rearrange("p k c -> p (k c)"), in_=wfg_d[l]), w=["wfg"], dma=True)
            for tg in range(4):
                tsl = slice(tg * 512, (tg + 1) * 512)
                for kc in range(8):
                    P.add("pe", lambda h, kc=kc, tsl=tsl: h.matmul(PS[7][0:8, :], wfg[:, kc, :], hT[:, kc, tsl],
                                                                   start=(kc == 0), stop=(kc == 7)),
                          r=["wfg", ("hT", tg)], w=[psk(7)])
                P.add("act", lambda h, tsl=tsl: h.activation(out=cumv[0:8, tsl], in_=PS[7][0:8, :], func=AF.Exp,
                                                             scale=-1.0, bias=negbf[:, 0:1]),
                      r=[psk(7), "negbf"], w=["cum"])
            P.add("act", lambda h: h.activation(out=cumv[0:8, :], in_=cumv[0:8, :], func=AF.Ln, scale=1.0, bias=1.0),
                  r=["cum"], w=["cum"])

        def fg_scan(l):
            P.add("dve", lambda h: h.tensor_tensor_scan(out=cumv[0:8, :], data0=ones8[:, 0:1].broadcast_to([8, S]),
                                                        data1=cumv[0:8, :], initial=0.0, op0=ALU.mult, op1=ALU.add),
                  r=["cum", "ones8"], w=["cum"])

        ffn_pre = {}

        def ffn_blk_dma(l, blk):
            ws = wslot()
            P.add("pool", lambda h, ws=ws, blk=blk: h.dma_start(
                out=WS[ws][:].rearrange("p (a b) -> p a b", a=2),
                in_=wgu_d[l, blk].rearrange("p (a b) -> p a b", a=2)),
                w=[("w", ws)], dma=True)
            return ws

        def ffn_gateup(l, hh, modnext=False, hook=None):
            bctr = 0
            for blk in range(11):
                if modnext and 1 <= blk <= 8:
                    mod_blocks(l + 1, [blk - 1])
                if hook and blk in hook:
                    hook[blk]()
                if (l, hh, blk) in ffn_pre:
                    ws = ffn_pre.pop((l, hh, blk))
                else:
                    ws = ffn_blk_dma(l, blk)
                wv = WS[ws][:].rearrange("p (k c) -> p k c", k=8)
                for ci in range(2):
                    c = blk * 2 + ci
                    for tgl in range(2):
                        tg = 2 * hh + tgl
                        tsl = slice(tg * 512, (tg + 1) * 512)
                        gb = 2 * (bctr % 3)
                        bctr += 1
                        for kc in range(8):
                            P.add("pe", lambda h, kc=kc, gb=gb, ci=ci, tsl=tsl, wv=wv: h.matmul(
                                PS[gb][:], wv[:, kc, ci * 128:(ci + 1) * 128], hT[:, kc, tsl], start=(kc == 0), stop=(kc == 7)),
                                r=[("w", ws), ("hT", tg)], w=[psk(gb)])
                        for kc in range(8):
                            P.add("pe", lambda h, kc=kc, gb=gb, ci=ci, tsl=tsl, wv=wv: h.matmul(
                                PS[gb + 1][:], wv[:, kc, 256 + ci * 128:256 + (ci + 1) * 128], hT[:, kc, tsl],
                                start=(kc == 0), stop=(kc == 7)),
                                r=[("w", ws), ("hT", tg)], w=[psk(gb + 1)])
                        sg = SG[bctr % 2]
                        sgk = ("sg", bctr % 2)
                        P.add("act", lambda h, gb=gb, sg=sg: h.activation(out=sg[:], in_=PS[gb][:], func=AF.Silu),
                              r=[psk(gb)], w=[sgk])
                        P.add("dve", lambda h, gb=gb, sg=sg, c=c, tgl=tgl: h.tensor_tensor(
                            out=actT[:, c, tgl * 512:(tgl + 1) * 512], in0=sg[:], in1=PS[gb + 1][:], op=ALU.mult),
                            r=[psk(gb + 1), sgk], w=[("actT", c)])

        def ffn_down(l, hh, extra=None):
            extra = collections.deque(extra or ())
            for oc in range(8):
                wdi = oc % 2
                wdf = WD[wdi]
                wd = wdf.rearrange("p (c n) -> p c n", c=NFC)
                P.add("pool", lambda h, wdf=wdf, oc=oc: h.dma_start(
                    out=wdf.rearrange("p (a b) -> p a b", a=2), in_=wdn_d[l, oc].rearrange("p (a b) -> p a b", a=2)),
                    w=[("wd", wdi)] + WD_OVERLAP, dma=True)
                for tgl in range(2):
                    tg = 2 * hh + tgl
                    tsl = slice(tg * 512, (tg + 1) * 512)
                    yb = 6 + (oc * 2 + tgl) % 2
                    for c in range(NFC):
                        P.add("pe", lambda h, yb=yb, c=c, wd=wd, tgl=tgl: h.matmul(
                            PS[yb][:], wd[:, c, :], actT[:, c, tgl * 512:(tgl + 1) * 512], start=(c == 0), stop=(c == NFC - 1)),
                            r=[("wd", wdi), ("actT", c)], w=[psk(yb)])
                    P.add("dve", lambda h, yb=yb, oc=oc, tsl=tsl: h.scalar_tensor_tensor(
                        out=xT[:, oc, tsl], in0=PS[yb][:], scalar=modT[l][:, 40 + oc:41 + oc], in1=xT[:, oc, tsl],
                        op0=ALU.mult, op1=ALU.add),
                        r=[psk(yb)] + mk(l, 40, 48), w=[("xT", tg), ("xTo", tg, oc)])
                    if l == depth - 1:
                        outs.append(P.add("sp", lambda h, oc=oc, tsl=tsl: h.dma_start(
                            out=outT_d[oc * 128:(oc + 1) * 128, tsl], in_=xT[:, oc, tsl]),
                            r=[("xTo", tg, oc)], dma=True))
                    if extra:
                        extra.popleft()()
            while extra:
                extra.popleft()()

        def layer(l):
            lam_init = 0.8 - 0.6 * math.exp(-0.3 * l)
            ppl = pp[:, l, :]
            P.add("dve", lambda h: h.tensor_copy(out=qkgs[:], in_=ppl[:, O_QKG:O_QKG + 4]), r=["pp"], w=["qkgs"])
            P.add("dve", lambda h: h.tensor_scalar(out=qkgs[:, 0:1], in0=ppl[:, O_QKG:O_QKG + 1], scalar1=0.125, scalar2=None, op0=ALU.mult),
                  r=["pp"], w=["qkgs"])
            P.add("dve", lambda h: h.tensor_scalar(out=qkgs[:, 2:3], in0=ppl[:, O_QKG + 2:O_QKG + 3], scalar1=0.125, scalar2=None, op0=ALU.mult),
                  r=["pp"], w=["qkgs"])
            P.add("dve", lambda h: h.tensor_scalar(out=negbf[:], in0=ppl[0:8, O_BF:O_BF + 1], scalar1=-1.0, scalar2=None, op0=ALU.mult),
                  r=["pp"], w=["negbf"])
            P.add("dve", lambda h: h.tensor_scalar(out=gnb[:], in0=ppl[:, O_GN:O_GN + 128], scalar1=(1.0 - lam_init), scalar2=None, op0=ALU.mult),
                  r=["pp"], w=["gnb"])
            for i in range(2):
                P.add("dve", lambda h, i=i: h.tensor_tensor(out=lamp[:], in0=ppl[:, O_LAM + 128 * i:O_LAM + 128 * i + 64],
                                                            in1=ppl[:, O_LAM + 128 * i + 64:O_LAM + 128 * i + 128], op=ALU.mult),
                      r=["pp"], w=["lamp"])
                P.add("dve", lambda h, i=i: h.tensor_reduce(out=lamt[:, i:i + 1], in_=lamp[:], axis=AX.X, op=ALU.add),
                      r=["lamp"], w=["lamt"])
            P.add("act", lambda h: h.activation(out=lamt[:, 2:4], in_=lamt[:, 0:2], func=AF.Exp), r=["lamt"], w=["lamt"])
            P.add("dve", lambda h: h.scalar_tensor_tensor(out=neglam[:], in0=lamt[:, 3:4], scalar=-lam_init, in1=lamt[:, 2:3],
                                                         op0=ALU.add, op1=ALU.subtract), r=["lamt"], w=["neglam"])


            P.barrier(BAR_ENGS)
            P.add("dve", lambda h: h.memset(Vf[:, :, :, 64:65], 1.0), w=["V"])
            P.add("dve", lambda h: h.memset(Vd[:, :, 128:129], 1.0), w=["V"])
            for s_ in range(2):
                P.add("pool", lambda h, s_=s_: h.dma_start(out=AUG[s_][0][68:72, :], in_=onesrow_d[0:4, :]),
                      w=[("aug", s_)] + FFN_KEYS, dma=True)
                P.add("pool", lambda h, s_=s_: h.dma_start(out=AUG[s_][1][64:68, :], in_=onesrow_d[4:8, :]),
                      w=[("aug", s_)] + FFN_KEYS, dma=True)
            fg_part1(l)

            def winp_dma(pidx):
                ws = wslot()
                P.add("pool", lambda h, ws=ws, pidx=pidx: h.dma_start(
                    out=WS[ws][:, 0:3072].rearrange("p (a b) -> p a b", a=2),
                    in_=winp_d[l, pidx].rearrange("p (a b) -> p a b", a=2)),
                    w=[("w", ws)], dma=True)
                return ws

            ws_next = winp_dma(0)
            for pidx in range(8):
                fox = pidx < 4
                pr = pidx if fox else pidx - 4
                ws = ws_next
                wv = WS[ws][:, 0:3072].rearrange("p (k c) -> p k c", k=8)
                combos = [(wi, tg) for wi in range(2) for tg in range(4)]
                if pidx == 0:
                    fg_scan(l)
                QL = 2
                for i in range(len(combos) + QL):
                    if pidx == 0 and i in (2, 4, 6, 8):
                        split_rows(8, cums_d, [i // 2 - 1])
                    if i >= QL:
                        ic = i - QL
                        wi, tg = combos[ic]
                        tsl = slice(tg * 512, (tg + 1) * 512)
                        ub = ic % 3
                        sq = SQ[ic % 2]
                        ptile, pkey = (pairq, "pairq") if wi == 0 else (pairk, "pairk")
                        gcol = (0 if fox else 2) + wi
                        tk = ic % 2
                        T = TMP[tk]
                        ssb = 5 + tk
                        P.add("pe", lambda h, sq=sq, ssb=ssb: h.matmul(PS[ssb][:], bdones, sq[:], start=True, stop=True),
                              r=[("sq", ic % 2), "cst"], w=[psk(ssb)])
                        P.add("act", lambda h, T=T, ssb=ssb: h.activation(out=T[:], in_=PS[ssb][:], func=AF.Ln, scale=1.0 / 64, bias=EPS),
                              r=[psk(ssb)], w=[("tmp", tk)])
                        P.add("act", lambda h, T=T: h.activation(out=T[:], in_=T[:], func=AF.Exp, scale=-0.5),
                              r=[("tmp", tk)], w=[("tmp", tk)])
                        P.add("dve", lambda h, ub=ub, tsl=tsl, ptile=ptile, gcol=gcol, T=T: h.scalar_tensor_tensor(
                            out=ptile[:, tsl], in0=PS[ub][:], scalar=qkgs[:, gcol:gcol + 1], in1=T[:],
                            op0=ALU.mult, op1=ALU.mult),
                            r=[psk(ub), ("tmp", tk), "qkgs"], w=[pkey])
                    if i < len(combos):
                        wi, tg = combos[i]
                        tsl = slice(tg * 512, (tg + 1) * 512)
                        ub = i % 3
                        for kc in range(8):
                            P.add("pe", lambda h, kc=kc, ub=ub, wi=wi, tsl=tsl, wv=wv: h.matmul(
                                PS[ub][:], wv[:, kc, wi * 128:(wi + 1) * 128], hT[:, kc, tsl], start=(kc == 0), stop=(kc == 7)),
                                r=[("w", ws), ("hT", tg)], w=[psk(ub)])
                        sq = SQ[i % 2]
                        P.add("act", lambda h, ub=ub, sq=sq: h.activation(out=sq[:], in_=PS[ub][:], func=AF.Square),
                              r=[psk(ub)], w=[("sq", i % 2)])
                for t4 in range(4):
                    vb = 3 + t4 % 2
                    for tt in range(4):
                        tok = (t4 * 4 + tt) * 128
                        for kc in range(8):
                            P.add("pe", lambda h, kc=kc, vb=vb, tt=tt, tok=tok, wv=wv: h.matmul(
                                PS[vb][:, tt * 128:(tt + 1) * 128], hT[:, kc, tok:tok + 128], wv[:, kc, 256:384],
                                start=(kc == 0), stop=(kc == 7)),
                                r=[("w", ws), ("hT", tok // 512)], w=[psk(vb)])
                    if fox:
                        P.add("dve", lambda h, vb=vb, t4=t4: h.tensor_copy(
                            out=Vf[:, t4 * 4:(t4 + 1) * 4, :, 0:64],
                            in_=PS[vb][:].rearrange("p (t h d) -> p t h d", t=4, h=2)),
                            r=[psk(vb)], w=["V"])
                    else:
                        P.add("dve", lambda h, vb=vb, t4=t4: h.tensor_copy(
                            out=Vd[:, t4 * 4:(t4 + 1) * 4, 0:128],
                            in_=PS[vb][:].rearrange("p (t d) -> p t d", t=4)),
                            r=[psk(vb)], w=["V"])
                for sub in range(2):
                    aslot = sub
                    Qa, Ka = AUG[aslot]
                    if fox:
                        head = 2 * pr + sub
                        bsrc = cums_d[head]
                        bkey = ("splitd", id(cums_d))
                    else:
                        head = pr
                        bsrc = alsp_d[head]
                        bkey = ("splitd", id(alsp_d))
                    P.add("sp", lambda h, Qa=Qa, sub=sub: h.dma_start(out=Qa[0:64, :], in_=pairq[sub * 64:(sub + 1) * 64, :]),
                          r=["pairq"], w=[("aug", aslot)], dma=True)
                    P.add("sp", lambda h, Ka=Ka, sub=sub: h.dma_start(out=Ka[0:64, :], in_=pairk[sub * 64:(sub + 1) * 64, :]),
                          r=["pairk"], w=[("aug", aslot)], dma=True)
                    P.add("sp", lambda h, Qa=Qa, bsrc=bsrc: h.dma_start(out=Qa[64:68, :], in_=bsrc),
                          r=[bkey], w=[("aug", aslot)], dma=True)
                    P.add("sp", lambda h, Ka=Ka, bsrc=bsrc: h.dma_start(out=Ka[68:72, :], in_=bsrc),
                          r=[bkey], w=[("aug", aslot)], dma=True)

                if pidx < 7:
                    ws_next = winp_dma(pidx + 1)
                mod_mis = [mod_dma(l, 8 + 2 * pidx), mod_dma(l, 9 + 2 * pidx)]
                if pidx in (3, 7):
                    ws2 = wslot()
                    wo = WS[ws2][:, 0:4096].rearrange("p (k c) -> p k c", k=4)
                    P.add("pool", lambda h, wo=wo, pidx=pidx: h.dma_start(
                        out=wo, in_=w_out_d[l][(0 if pidx == 3 else 1) * 512:((0 if pidx == 3 else 1) + 1) * 512, :].rearrange("(k p) c -> p k c", p=128)),
                        w=[("w", ws2)], dma=True)
                if pidx == 7:
                    ffn_pre[(l, 0, 0)] = ffn_blk_dma(l, 0)

                units = []
                for sub in range(2):
                    if fox:
                        head = 2 * pr + sub

                        def evac(j, ob, head=head):
                            O = PS[ob[0]][:, 0:260].rearrange("p (q d) -> p q d", q=4)
                            P.add("dve", lambda h, O=O: h.reciprocal(out=rc[:, 0:4], in_=O[:, :, 64]),
                                  r=[psk(ob[0])], w=["rc"])
                            for qb in range(4):
                                P.add("dve", lambda h, O=O, qb=qb, j=j: h.tensor_scalar(
                                    out=mixed[:, 4 * j + qb, head * 64:(head + 1) * 64], in0=O[:, qb, 0:64],
                                    scalar1=rc[:, qb:qb + 1], scalar2=None, op0=ALU.mult),
                                    r=[psk(ob[0]), "rc"], w=[("mixed", 4 * j + qb)])

                        units.append((sub, "f", sub, evac))
                    else:
                        head = pr
                        if sub == 0:
                            def evac(j, ob, head=head):
                                for qb in range(4):
                                    O = PS[ob[qb // 2]][:, 0:258].rearrange("p (q d) -> p q d", q=2)
                                    q2 = qb % 2
                                    P.add("dve", lambda h, O=O, q2=q2, qb=qb: h.reciprocal(out=rc[:, qb:qb + 1], in_=O[:, q2, 128:129]),
                                          r=[psk(ob[qb // 2])], w=["rc"])
                                    P.add("dve", lambda h, O=O, q2=q2, qb=qb, j=j: h.tensor_scalar(
                                        out=t1[:, 4 * j + qb, :], in0=O[:, q2, 0:128], scalar1=rc[:, qb:qb + 1], scalar2=None,
                                        op0=ALU.mult),
                                        r=[psk(ob[qb // 2]), "rc"], w=["cum"])
                        else:
                            def evac(j, ob, head=head):
                                for qb in range(4):
                                    O = PS[ob[qb // 2]][:, 0:258].rearrange("p (q d) -> p q d", q=2)
                                    q2 = qb % 2
                                    P.add("dve", lambda h, O=O, q2=q2, qb=qb: h.reciprocal(out=rc[:, qb:qb + 1], in_=O[:, q2, 128:129]),
                                          r=[psk(ob[qb // 2])], w=["rc"])
                                    P.add("dve", lambda h, qb=qb: h.tensor_scalar(
                                        out=rc[:, 4 + qb:5 + qb], in0=rc[:, qb:qb + 1], scalar1=neglam[:, 0:1], scalar2=None, op0=ALU.mult),
                                        r=["rc", "neglam"], w=["rc"])
                                    P.add("dve", lambda h, O=O, q2=q2, qb=qb, j=j: h.scalar_tensor_tensor(
                                        out=t1[:, 4 * j + qb, :], in0=O[:, q2, 0:128], scalar=rc[:, 4 + qb:5 + qb],
                                        in1=t1[:, 4 * j + qb, :], op0=ALU.mult, op1=ALU.add),
                                        r=[psk(ob[qb // 2]), "rc"], w=["cum"])
                                    P.add("dve", lambda h, qb=qb, j=j: h.scalar_tensor_tensor(
                                        out=junk[:], in0=t1[:, 4 * j + qb, :], scalar=1.0, in1=t1[:, 4 * j + qb, :],
                                        op0=ALU.mult, op1=ALU.mult, accum_out=ssd[:, qb:qb + 1]),
                                        r=["cum"], w=["ssd", "junk"])
                                P.add("dve", lambda h: h.tensor_scalar(out=ssd[:, 4:8], in0=ssd[:, 0:4], scalar1=1.0 / 128, scalar2=EPS,
                                                                       op0=ALU.mult, op1=ALU.add),
                                      r=["ssd"], w=["ssd"])
                                P.add("pool", lambda h: h.tensor_tensor(out=ssd[:, 4:8], in0=ssd[:, 4:8], in1=mhalf[:, 0:4], op=ALU.pow),
                                      r=["ssd", "mhalf"], w=["ssd"])
                                for qb in range(4):
                                    P.add("dve", lambda h, qb=qb, j=j: h.scalar_tensor_tensor(
                                        out=mixed[:, 4 * j + qb, head * 128:(head + 1) * 128], in0=t1[:, 4 * j + qb, :],
                                        scalar=ssd[:, 4 + qb:5 + qb], in1=gnb[:], op0=ALU.mult, op1=ALU.mult),
                                        r=["cum", "ssd", "gnb"], w=[("mixed", 4 * j + qb)])

                        units.append((sub, "d", 0, evac))

                attention_units(units)

                mod_blocks(l, [8 + 2 * pidx, 9 + 2 * pidx], mod_mis)

                if pidx in (3, 7):
                    hf = 0 if pidx == 3 else 1
                    xnb = [XN[0][:].bitcast(BF16), XN[1][:].bitcast(BF16)]
                    MT = [[PT[0], PT[1], PT[2], PT[3]],
                          [xnb[0][:, 0:512], xnb[0][:, 512:1024], xnb[1][:, 0:512], xnb[1][:, 512:1024]]]
                    MTK = [[("pt", 0), ("pt", 1), ("pt", 2), ("pt", 3)], [("xn", 0), ("xn", 0), ("xn", 1), ("xn", 1)]]
                    TB = [(0, 1), (5, 6)]

                    def op_transposes(tg):
                        par = tg % 2
                        for fc in range(4):
                            tb = TB[par][fc % 2]
                            tpv = PS[tb][:].bitcast(BF16)
                            for tt in range(4):
                                P.add("pe", lambda h, tpv=tpv, tt=tt, fc=fc, tg=tg: h.transpose(
                                    tpv[:, tt * 128:(tt + 1) * 128], mixed[:, 4 * tg + tt, fc * 128:(fc + 1) * 128], ident),
                                    r=[("mixed", 4 * tg + tt), "cst"], w=[psk(tb)])
                            mt = MT[par][fc]
                            if fc % 2 == 0:
                                P.add("act", lambda h, tpv=tpv, mt=mt: h.activation(out=mt, in_=tpv[:, 0:512], func=AF.Copy),
                                      r=[psk(tb)], w=[MTK[par][fc]])
                            else:
                                P.add("dve", lambda h, tpv=tpv, mt=mt: h.tensor_copy(out=mt, in_=tpv[:, 0:512]),
                                      r=[psk(tb)], w=[MTK[par][fc]])

                    op_transposes(0)
                    for tg in range(4):
                        tsl = slice(tg * 512, (tg + 1) * 512)
                        par = tg % 2
                        if tg + 1 < 4:
                            op_transposes(tg + 1)
                        for oc in range(8):
                            yb = 2 + oc % 3
                            for fc in range(4):
                                P.add("pe", lambda h, yb=yb, fc=fc, oc=oc, wo=wo, par=par: h.matmul(
                                    PS[yb][:], wo[:, fc, oc * 128:(oc + 1) * 128], MT[par][fc], start=(fc == 0), stop=(fc == 3)),
                                    r=[("w", ws2), MTK[par][fc]], w=[psk(yb)])
                            P.add("dve", lambda h, yb=yb, oc=oc, tsl=tsl: h.scalar_tensor_tensor(
                                out=xT[:, oc, tsl], in0=PS[yb][:], scalar=modT[l][:, 16 + oc:17 + oc], in1=xT[:, oc, tsl],
                                op0=ALU.mult, op1=ALU.add),
                                r=[psk(yb)] + mk(l, 16, 24), w=[("xT", tg)])
                        if hf == 1 and tg < 2:
                            rms_p1(tg)

            mod_a(l, 2)
            P.barrier(BAR_ENGS)
            nxt = l + 1 < depth
            rms_p2(l, 0, a2[l], ("a2", l), 24)
            rms_p2(l, 1, a2[l], ("a2", l), 24)
            ffn_gateup(l, 0, modnext=nxt, hook={6: lambda: rms_p1(2), 8: lambda: rms_p1(3)})
            if nxt:
                mod_a(l + 1, 1)
            ffn_down(l, 0, extra=rms_p2_thunks(l, 2, a2[l], ("a2", l), 24) + rms_p2_thunks(l, 3, a2[l], ("a2", l), 24))
            ffn_gateup(l, 1, hook=({6: lambda: rms_p1(0), 8: lambda: rms_p1(1)} if nxt else None))
            ffn_down(l, 1, extra=(rms_p2_thunks(l + 1, 0, a1[l + 1], ("a1", l + 1), 0)
                                  + rms_p2_thunks(l + 1, 1, a1[l + 1], ("a1", l + 1), 0)) if nxt else None)
            if nxt:
                rms_seq(l + 1, (2, 3), a1[l + 1], ("a1", l + 1), 0)

        outs = []
        prologue()
        for l in range(depth):
            layer(l)

        ops = P.emit(sems, dsems)
        for oi in outs:
            op = ops[oi]
            nc.sync.wait_ge(op["dsem"], op["dval"])
    return nc


_NC_CACHE = {}


def _host_inputs(inp):
    f = lambda a: np.ascontiguousarray(np.asarray(a, dtype=np.float32))
    x = f(inp["x"]); c = f(inp["c"])
    B = x.shape[0]
    pp = np.zeros((DEPTH, 128, NP_), np.float32)
    for l in range(DEPTH):
        pp[l, :, O_LN1:O_LN1 + 8] = f(inp["ln1_g"])[l].reshape(8, 128).T
        pp[l, :, O_LN2:O_LN2 + 8] = f(inp["ln2_g"])[l].reshape(8, 128).T
        pp[l, :, O_BADA:O_BADA + 48] = f(inp["b_ada"])[l].reshape(48, 128).T
        pp[l, :, O_QKG + 0] = np.tile(f(inp["fox_qk_g"])[l, 0], 2)
        pp[l, :, O_QKG + 1] = np.tile(f(inp["fox_qk_g"])[l, 1], 2)
        pp[l, :, O_QKG + 2] = np.tile(f(inp["diff_qk_g"])[l, 0], 2)
        pp[l, :, O_QKG + 3] = np.tile(f(inp["diff_qk_g"])[l, 1], 2)
        pp[l, :, O_LAM:O_LAM + 256] = np.broadcast_to(f(inp["diff_lam"])[l].reshape(1, 256), (128, 256))
        pp[l, :, O_GN:O_GN + 128] = np.broadcast_to(f(inp["diff_norm_g"])[l].reshape(1, 128), (128, 128))
        pp[l, 0:8, O_BF] = f(inp["b_f"])[l]
    cst = np.zeros((128, 512), np.float32)
    cst[:, 0:128] = np.eye(128, dtype=np.float32)
    kk = np.arange(128)[:, None]; qq = np.arange(128)[None, :]
    cst[:, 128:256] = np.where(kk <= qq, 0.0, -30000.0)
    cst[0:64, 256:320] = 1.0
    cst[64:128, 320:384] = 1.0
    cst[:, 384:512] = 1.0
    slopes = np.array([2.0 ** (-8.0 * (h + 1) / 4) for h in range(4)], np.float32)
    import ml_dtypes
    al = (slopes[:, None] * np.arange(S, dtype=np.float32)[None, :]).astype(np.float32)
    parts = []
    rem = al.copy()
    for _ in range(4):
        p_ = rem.astype(ml_dtypes.bfloat16).astype(np.float32)
        parts.append(p_)
        rem = (rem - p_).astype(np.float32)
    alibi = np.ascontiguousarray(np.stack(parts, 1))
    onesrow = np.concatenate([np.ones((4, S), np.float32), -np.ones((4, S), np.float32)], 0)
    w_ada = f(inp["w_ada"]); w_in = f(inp["w_in"]); w_gate = f(inp["w_gate"]); w_up = f(inp["w_up"]); w_down = f(inp["w_down"])
    wada = np.ascontiguousarray(w_ada.reshape(DEPTH, 8, 128, 24, 256).transpose(0, 3, 2, 1, 4)).reshape(DEPTH, 24, 128, 2048)
    winp = np.empty((DEPTH, 8, 128, 8, 384), np.float32)
    for pidx in range(8):
        base = 0 if pidx < 4 else 1544
        pr = pidx % 4
        for wi in range(3):
            c0 = base + wi * 512 + pr * 128
            winp[:, pidx, :, :, wi * 128:(wi + 1) * 128] = w_in[:, :, c0:c0 + 128].reshape(DEPTH, 8, 128, 128).transpose(0, 2, 1, 3)
    winp = winp.reshape(DEPTH, 8, 128, 3072)
    wfg = np.ascontiguousarray(w_in[:, :, 1536:1544].reshape(DEPTH, 8, 128, 8).transpose(0, 2, 1, 3)).reshape(DEPTH, 128, 64)
    wgu = np.empty((DEPTH, 11, 128, 8, 512), np.float32)
    wgu[..., 0:256] = w_gate.reshape(DEPTH, 8, 128, 11, 256).transpose(0, 3, 2, 1, 4)
    wgu[..., 256:512] = w_up.reshape(DEPTH, 8, 128, 11, 256).transpose(0, 3, 2, 1, 4)
    wgu = wgu.reshape(DEPTH, 11, 128, 4096)
    wdn = np.ascontiguousarray(w_down.reshape(DEPTH, NFC, 128, 8, 128).transpose(0, 3, 2, 1, 4)).reshape(DEPTH, 8, 128, DFF)
    shared = dict(pp=pp, cst=cst, alibi=alibi, onesrow=onesrow, wada=wada, winp=winp, wfg=wfg,
                  w_out=f(inp["w_out"]), wgu=wgu, wdn=wdn)
    maps = []
    for b in range(B):
        m = dict(shared)
        m["xT"] = np.ascontiguousarray(x[b].T)
        m["cT"] = np.ascontiguousarray(c[b].reshape(8, 128).T)
        maps.append(m)
    return maps


def kernel(**inputs):
    maps = _host_inputs(inputs)
    if "nc" not in _NC_CACHE:
        _NC_CACHE["nc"] = build()
    nc = _NC_CACHE["nc"]
    res = run_bass_kernel_spmd(nc, maps, core_ids=list(range(len(maps))))
    out = np.stack([np.ascontiguousarray(np.asarray(r["outT"], dtype=np.float32).T) for r in res.results], 0)
    return out
```

```python
import math
import collections
import contextlib
import numpy as np
import concourse.bass as bass
import concourse.mybir as mybir
from concourse.bass_utils import run_bass_kernel_spmd

F32 = mybir.dt.float32
BF16 = mybir.dt.bfloat16
AF = mybir.ActivationFunctionType
ALU = mybir.AluOpType
AX = mybir.AxisListType

D = 1024
S = 2048
DEPTH = 2
DFF = 2816
NFC = DFF // 128
INW = 3080
EPS = 1e-6
NP_ = 8 + 8 + 48 + 4 + 256 + 128 + 1
O_LN1, O_LN2, O_BADA, O_QKG, O_LAM, O_GN, O_BF = 0, 8, 16, 64, 68, 324, 452

ENGS = ("pe", "act", "dve", "pool", "sp")


class Prog:
    def __init__(self, nc):
        self.nc = nc
        self.ops = []
        self.lastw = {}
        self.rd_c = {}
        self.rd_d = {}
        self.ecount = {e: 0 for e in ENGS}
        self.pending = {e: set() for e in ENGS}

    def barrier(self, engines=ENGS):
        lastc = {}
        lastd = {"sp": [], "pool": []}
        for op in self.ops:
            if op["dma"]:
                lastd[op["eng"]].append(op["idx"])
            else:
                lastc[op["eng"]] = op["idx"]
        deps = set(lastc.values())
        for q in lastd:
            deps.update(lastd[q][-8:])
        for e in engines:
            self.pending[e] = set(deps)

    def add(self, eng, fn, r=(), w=(), dma=False):
        idx = len(self.ops)
        deps = set()
        if self.pending[eng]:
            deps |= self.pending[eng]
            self.pending[eng] = set()
        for k in r:
            if k in self.lastw:
                deps.add(self.lastw[k])
        for k in w:
            if k in self.lastw:
                deps.add(self.lastw[k])
            for v in self.rd_c.get(k, {}).values():
                deps.add(v)
            for v in self.rd_d.get(k, ()):
                deps.add(v)
        for k in w:
            self.lastw[k] = idx
            self.rd_c[k] = {}
            self.rd_d[k] = []
        ws = set(w)
        for k in r:
            if k in ws:
                continue
            if dma:
                self.rd_d.setdefault(k, []).append(idx)
            else:
                self.rd_c.setdefault(k, {})[eng] = idx
        deps.discard(idx)
        self.ops.append(dict(eng=eng, fn=fn, dma=dma, idx=idx, eidx=self.ecount[eng], deps=deps, ms=False))
        self.ecount[eng] += 1
        return idx

    def emit(self, sems, dsems):
        nc = self.nc
        H = {"pe": nc.tensor, "act": nc.scalar, "dve": nc.vector, "pool": nc.gpsimd, "sp": nc.sync}
        ops = self.ops
        dcnt = {q: 0 for q in dsems}
        dtot = {}
        for op in ops:
            if op["dma"]:
                q = op["eng"]
                pool = dsems[q]
                si = dcnt[q] % len(pool)
                dcnt[q] += 1
                prev = dtot.get((q, si), 0)
                op["dsem"] = pool[si]
                op["dkey"] = (q, si)
                op["dprev"] = prev
                op["dval"] = prev + 16
                dtot[(q, si)] = prev + 16
        for op in ops:
            for di in op["deps"]:
                y = ops[di]
                if y["dma"]:
                    continue
                if y["eng"] != op["eng"]:
                    y["ms"] = True
                elif op["eng"] != "pe" and (op["eidx"] - y["eidx"] <= 3):
                    y["ms"] = True
        mcount = {e: 0 for e in ENGS}
        for op in ops:
            if op["ms"]:
                mcount[op["eng"]] += 1
                op["msval"] = mcount[op["eng"]]
        waited = {e: {} for e in ENGS}
        nwait = 0
        for op in ops:
            e = op["eng"]
            h = H[e]
            need = {}
            for di in op["deps"]:
                y = ops[di]
                if y["dma"]:
                    key = ("d",) + y["dkey"]
                    need[key] = max(need.get(key, 0), y["dval"])
                else:
                    if y["eng"] == e and (e == "pe" or op["eidx"] - y["eidx"] > 3):
                        continue
                    key = ("c", y["eng"])
                    need[key] = max(need.get(key, 0), y["msval"])
            if op["dma"] and op["dprev"] > 0:
                key = ("d",) + op["dkey"]
                need[key] = max(need.get(key, 0), op["dprev"])
            for key, val in need.items():
                if waited[e].get(key, 0) >= val:
                    continue
                waited[e][key] = val
                sem = sems[key[1]] if key[0] == "c" else dsems[key[1]][key[2]]
                h.wait_ge(sem, val)
                nwait += 1
            ins = op["fn"](h)
            if op["dma"]:
                ins.then_inc(op["dsem"], 16)
            elif op["ms"]:
                ins.then_inc(sems[e], 1)
        self.nwait = nwait
        return ops


def build(depth=DEPTH):
    nc = bass.Bass("TRN2", target_bir_lowering=False)

    def din(name, shape):
        return nc.dram_tensor(name, shape, F32, kind="ExternalInput").ap()

    xT_d = din("xT", [D, S])
    cT_d = din("cT", [128, 8])
    pp_d = din("pp", [DEPTH, 128, NP_])
    cst_d = din("cst", [128, 512])
    alibi_d = din("alibi", [4, 4, S])
    onesrow_d = din("onesrow", [8, S])
    wada_d = din("wada", [DEPTH, 24, 128, 2048])
    winp_d = din("winp", [DEPTH, 8, 128, 3072])
    wfg_d = din("wfg", [DEPTH, 128, 64])
    w_out_d = din("w_out", [DEPTH, D, D])
    wgu_d = din("wgu", [DEPTH, 11, 128, 4096])
    wdn_d = din("wdn", [DEPTH, 8, 128, DFF])
    outT_d = nc.dram_tensor("outT", [D, S], F32, kind="ExternalOutput").ap()
    cums_d = nc.dram_tensor("cums", [8, 4, S], BF16, kind="Internal").ap()
    alsp_d = nc.dram_tensor("alsp", [4, 4, S], BF16, kind="Internal").ap()

    es = contextlib.ExitStack()
    with es:
        def sb(name, shape, dt):
            return es.enter_context(nc.sbuf_tensor(name, shape, dt))

        xT = sb("xT_sb", [128, 8, S], F32)
        hT = sb("hT", [128, 8, S], BF16)
        WS = [sb(f"ws{i}", [128, 4096], BF16) for i in range(3)]
        MW = [sb(f"mw{i}", [128, 2048], BF16) for i in range(2)]
        RCOLS = 30848
        R = sb("R", [128, RCOLS], BF16)
        o = 0
        AUG = []
        for s_ in range(2):
            q_ = R[:, o:o + 2048]; o += 2048
            k_ = R[:, o:o + 2048]; o += 2048
            AUG.append((q_, k_))
        pairq = R[:, o:o + 2048]; o += 2048
        pairk = R[:, o:o + 2048]; o += 2048
        Vf = R[:, o:o + 16 * 130].rearrange("p (t h d) -> p t h d", t=16, h=2); o += 2112
        Vd = R[:, o:o + 16 * 129].rearrange("p (t d) -> p t d", t=16); o += 2112
        PTall = R[:, o:o + 2048]; o += 2048
        PT = [PTall[:, i * 512:(i + 1) * 512] for i in range(4)]
        mixed = R[:, o:o + 8192].rearrange("p (t c) -> p t c", t=16); o += 8192
        cumt1_b = R[:, o:o + 4096]; o += 4096
        cumv = cumt1_b.bitcast(F32)
        t1 = cumv.rearrange("p (t d) -> p t d", t=16)
        att_end = o
        o = 0
        actT = R[:, o:o + NFC * 1024].rearrange("p (c t) -> p c t", c=NFC); o += NFC * 1024
        WD = []
        for i in range(2):
            WD.append(R[:, o:o + NFC * 128]); o += NFC * 128
        SG = []
        for i in range(2):
            SG.append(R[:, o:o + 1024].bitcast(F32)); o += 1024
        assert max(att_end, o) <= RCOLS, (att_end, o)
        WD_OVERLAP = [("mixed", t) for t in range(16)] + ["cum"]
        FFN_KEYS = [("actT", c) for c in range(NFC)] + [("wd", 0), ("wd", 1), ("sg", 0), ("sg", 1)]

        tmpA = sb("tmpA", [128, 512], F32)
        tmpB = sb("tmpB", [128, 512], F32)
        TMP = [tmpA, tmpB]
        XN = [sb(f"xn{i}", [128, 512], F32) for i in range(2)]
        SQ = [sb(f"sq{i}", [128, 512], BF16) for i in range(2)]
        cst = sb("cst_bf", [128, 512], BF16)
        ident = cst[:, 0:128]
        maskb = cst[:, 128:256]
        bdones = cst[:, 256:384]
        ones128 = cst[:, 384:512]
        zt = sb("zt", [128, 512], BF16)
        c_sb = sb("c_sb", [128, 8], F32)
        condT = sb("condT", [128, 8], BF16)
        pp = sb("pp_sb", [128, DEPTH, NP_], F32)
        modT = [sb(f"modT{l}", [128, 48], F32) for l in range(DEPTH)]
        a1 = [sb(f"a1_{l}", [128, 8], F32) for l in range(DEPTH)]
        a2 = [sb(f"a2_{l}", [128, 8], F32) for l in range(DEPTH)]
        qkgs = sb("qkgs", [128, 4], F32)
        negbf = sb("negbf", [8, 1], F32)
        ones8 = sb("ones8", [8, 1], F32)
        neglam = sb("neglam", [128, 1], F32)
        lamt = sb("lamt", [128, 4], F32)
        lamp = sb("lamp", [128, 64], F32)
        gnb = sb("gnb", [128, 128], F32)
        rc = sb("rc", [128, 8], F32)
        ssd = sb("ssd", [128, 8], F32)
        junk = sb("junk", [128, 128], BF16)
        mhalf = sb("mhalf", [128, 4], F32)
        wfg = sb("wfg_sb", [128, 8, 8], BF16)

        PS = [es.enter_context(nc.psum_tensor(f"ps{i}", [128, 512], F32)) for i in range(8)]

        sems = {e: es.enter_context(nc.semaphore(f"sem_{e}")) for e in ENGS}
        dsems = {
            "sp": [es.enter_context(nc.semaphore(f"dsp{i}")) for i in range(8)],
            "pool": [es.enter_context(nc.semaphore(f"dpl{i}")) for i in range(8)],
        }

        P = Prog(nc)
        wctr = [0]
        mwctr = [0]
        BAR_ENGS = ("pe", "act", "dve", "sp")

        def wslot():
            i = wctr[0] % 3
            wctr[0] += 1
            return i

        def psk(b):
            return ("ps", b)

        def mk(l, j0, j1):
            return [("modT", l, b) for b in range(j0 // 2, (j1 + 1) // 2)]

        P.add("sp", lambda h: h.dma_start(out=c_sb[:], in_=cT_d[:]), w=["c_sb"], dma=True)
        P.add("sp", lambda h: h.dma_start(out=pp[:], in_=pp_d.rearrange("l p n -> p l n")), w=["pp"], dma=True)
        def x_load(tg, q):
            P.add(q, lambda h, tg=tg: h.dma_start(
                out=xT[:, :, tg * 512:(tg + 1) * 512],
                in_=xT_d[:, tg * 512:(tg + 1) * 512].rearrange("(k p) t -> p k t", p=128)),
                w=[("xT", tg)], dma=True)

        x_load(0, "sp")
        P.add("pool", lambda h: h.dma_start(out=cst[:], in_=cst_d[:]), w=["cst"], dma=True)
        P.add("dve", lambda h: h.memset(zt[:], 0.0), w=["zt"])
        P.add("dve", lambda h: h.memset(ones8[:], 1.0), w=["ones8"])
        P.add("dve", lambda h: h.memset(mhalf[:], -0.5), w=["mhalf"])
        P.add("act", lambda h: h.activation(out=condT[:], in_=c_sb[:], func=AF.Silu), r=["c_sb"], w=["condT"])

        ptkeys = [("pt", i) for i in range(4)]

        def split_rows(nrows, dst_d, chunks=range(4)):
            spst = PTall[0:nrows, 0:2048].rearrange("p (a b) -> p a b", a=4)
            for c in chunks:
                xs = cumv[0:nrows, c * 512:(c + 1) * 512]
                for a in range(4):
                    P.add("dve", lambda h, xs=xs, a=a: h.tensor_copy(out=spst[:, a, :], in_=xs),
                          r=["cum"], w=ptkeys)
                    if a < 3:
                        P.add("dve", lambda h, xs=xs, a=a: h.tensor_tensor(out=xs, in0=xs, in1=spst[:, a, :], op=ALU.subtract),
                              r=ptkeys, w=["cum"])
                P.add("sp", lambda h, c=c: h.dma_start(out=dst_d[:, :, c * 512:(c + 1) * 512], in_=spst),
                      r=ptkeys, w=[("splitd", id(dst_d))], dma=True)

        def mod_dma(l, blk):
            mi = mwctr[0] % 2
            mwctr[0] += 1
            mw = MW[mi]
            P.add("pool", lambda h, mw=mw, blk=blk: h.dma_start(out=mw[:], in_=wada_d[l, blk]),
                  w=[("mw", mi)], dma=True)
            return mi

        def mod_blocks(l, blks, mis=None):
            for bi, blk in enumerate(blks):
                mi = mis[bi] if mis is not None else mod_dma(l, blk)
                mod_compute(l, blk, MW[mi][:], ("mw", mi))

        def mod_compute(l, blk, mwap, mwkey):
            if True:
                mi = mwkey
                wv = mwap.rearrange("p (k c) -> p k c", k=8)
                for cc in range(2):
                    for kc in range(8):
                        P.add("pe", lambda h, wv=wv, cc=cc, kc=kc: h.matmul(
                            PS[7][:, cc:cc + 1], wv[:, kc, cc * 128:(cc + 1) * 128], condT[:, kc:kc + 1],
                            start=(kc == 0), stop=(kc == 7)),
                            r=[mwkey, "condT"], w=[psk(7)])
                P.add("dve", lambda h, blk=blk: h.tensor_tensor(
                    out=modT[l][:, 2 * blk:2 * blk + 2], in0=PS[7][:, 0:2],
                    in1=pp[:, l, O_BADA + 2 * blk:O_BADA + 2 * blk + 2], op=ALU.add),
                    r=[psk(7), "pp"], w=[("modT", l, blk)])

        def mod_a(l, which):
            if which == 1:
                P.add("dve", lambda h: h.scalar_tensor_tensor(out=a1[l][:], in0=modT[l][:, 8:16], scalar=1.0,
                                                             in1=pp[:, l, O_LN1:O_LN1 + 8], op0=ALU.add, op1=ALU.mult),
                      r=mk(l, 8, 16) + ["pp"], w=[("a1", l)])
            else:
                P.add("dve", lambda h: h.scalar_tensor_tensor(out=a2[l][:], in0=modT[l][:, 32:40], scalar=1.0,
                                                             in1=pp[:, l, O_LN2:O_LN2 + 8], op0=ALU.add, op1=ALU.mult),
                      r=mk(l, 32, 40) + ["pp"], w=[("a2", l)])

        def prologue():
            x_load(1, "sp")
            bufs = [(MW[0][:], ("mw", 0)), (MW[1][:], ("mw", 1))]
            for i_ in range(3):
                bufs.append((WS[i_][:, 0:2048], ("w", i_)))
                bufs.append((WS[i_][:, 2048:4096], ("w", i_)))
            for blk in range(8):
                P.add("pool", lambda h, blk=blk: h.dma_start(out=bufs[blk][0], in_=wada_d[0, blk]),
                      w=[bufs[blk][1]], dma=True)
                if blk == 3:
                    x_load(2, "pool")
            x_load(3, "pool")
            P.add("pool", lambda h: h.dma_start(out=alsp_d, in_=alibi_d), w=[("splitd", id(alsp_d))], dma=True)
            rms_p1(0)
            rms_p1(1)
            for blk in range(8):
                mod_compute(0, blk, bufs[blk][0], bufs[blk][1])
            mod_a(0, 1)
            rms_p2(0, 0, a1[0], ("a1", 0), 0)
            rms_p1(2)
            rms_p2(0, 1, a1[0], ("a1", 0), 0)
            rms_p1(3)
            rms_p2(0, 2, a1[0], ("a1", 0), 0)
            rms_p2(0, 3, a1[0], ("a1", 0), 0)

        def rms_p1(tg):
            tsl = slice(tg * 512, (tg + 1) * 512)
            tk = tg % 2
            T = TMP[tk]
            for kc in range(8):
                wk = [("hTs", tg, kc)] + ([("hT", tg), ("hTg", tg)] if kc == 0 else [])
                rk = [("xT", tg)] + ([] if kc == 0 else [("hTg", tg)])
                if kc not in (1, 5):
                    P.add("act", lambda h, kc=kc: h.activation(out=hT[:, kc, tsl], in_=xT[:, kc, tsl], func=AF.Square),
                          r=rk, w=wk)
                else:
                    P.add("dve", lambda h, kc=kc: h.tensor_tensor(out=hT[:, kc, tsl], in0=xT[:, kc, tsl], in1=xT[:, kc, tsl], op=ALU.mult),
                          r=rk, w=wk)
            for kc in range(8):
                P.add("pe", lambda h, kc=kc: h.matmul(PS[7][:], ones128, hT[:, kc, tsl], start=(kc == 0), stop=(kc == 7)),
                      r=[("hTs", tg, kc), "cst"], w=[psk(7)])
            P.add("act", lambda h: h.activation(out=T[:], in_=PS[7][:], func=AF.Ln, scale=1.0 / D, bias=EPS),
                  r=[psk(7)], w=[("tmp", tk)])
            P.add("act", lambda h: h.activation(out=T[:], in_=T[:], func=AF.Exp, scale=-0.5),
                  r=[("tmp", tk)], w=[("tmp", tk)])

        def rms_p2_thunks(l, tg, a_t, akey, sh_col0):
            tsl = slice(tg * 512, (tg + 1) * 512)
            tk = tg % 2
            T = TMP[tk]
            shkeys = mk(l, sh_col0, sh_col0 + 8)
            th = []
            for kc in range(8):
                def one(kc=kc):
                    xn = XN[kc % 2]
                    P.add("dve", lambda h: h.scalar_tensor_tensor(
                        out=xn[:], in0=xT[:, kc, tsl], scalar=a_t[:, kc:kc + 1], in1=T[:], op0=ALU.mult, op1=ALU.mult),
                        r=[("xT", tg), ("tmp", tk), akey], w=[("xn", kc % 2)])
                    P.add("act", lambda h: h.activation(
                        out=hT[:, kc, tsl], in_=xn[:], func=AF.Identity, bias=modT[l][:, sh_col0 + kc:sh_col0 + kc + 1]),
                        r=[("xn", kc % 2)] + shkeys, w=[("hT", tg), ("hTs", tg, kc)])
                th.append(one)
            return th

        def rms_p2(l, tg, a_t, akey, sh_col0):
            for t_ in rms_p2_thunks(l, tg, a_t, akey, sh_col0):
                t_()

        def rms_seq(l, tgs, a_t, akey, sh_col0):
            tgs = list(tgs)
            rms_p1(tgs[0])
            for i_, tg in enumerate(tgs):
                if i_ + 1 < len(tgs):
                    rms_p1(tgs[i_ + 1])
                rms_p2(l, tg, a_t, akey, sh_col0)

        pvctr = [0]

        def attention_units(units):
            tasks = [(u, j, kb) for u in range(len(units)) for j in range(4) for kb in range(4 * j + 4)]
            LAG = 3
            SB = (0, 1, 2, 7)
            n = len(tasks)
            info = {}
            for s_ in range(n + LAG):
                if s_ < n:
                    u, j, kb = tasks[s_]
                    aslot = units[u][0]
                    Qa, Ka = AUG[aslot]
                    off = max(0, kb - 4 * j) * 128
                    sbk = SB[s_ % 4]
                    pts = s_ % 4
                    info[s_] = (sbk, pts, off)
                    diag = kb >= 4 * j
                    P.add("pe", lambda h, j=j, kb=kb, off=off, sbk=sbk, diag=diag, Qa=Qa, Ka=Ka: h.matmul(
                        PS[sbk][:, off:512], Ka[0:72, kb * 128:(kb + 1) * 128],
                        Qa[0:72, j * 512 + off:(j + 1) * 512], start=True, stop=(not diag)),
                        r=[("aug", aslot)], w=[psk(sbk)])
                    if diag:
                        P.add("pe", lambda h, off=off, sbk=sbk: h.matmul(
                            PS[sbk][:, off:off + 128], ident, maskb, start=False, stop=True),
                            r=["cst"], w=[psk(sbk)])
                    P.add("act", lambda h, off=off, sbk=sbk, pts=pts: h.activation(
                        out=PT[pts][:, off:512], in_=PS[sbk][:, off:512], func=AF.Exp),
                        r=[psk(sbk)], w=[("pt", pts)])
                if s_ >= LAG:
                    u, j, kb = tasks[s_ - LAG]
                    aslot, vkind, vidx, evac = units[u]
                    sbk, pts, off = info[s_ - LAG]
                    gq = (pvctr[0] + u) * 4 + j
                    if vkind == "f":
                        ob = [3 + gq % 2]
                    else:
                        ob = [3, 4] if gq % 2 == 0 else [5, 6]
                    if kb == 0:
                        if vkind == "f":
                            P.add("pe", lambda h, ob=ob: h.matmul(PS[ob[0]][:, 0:260], zt[:, 0:128], zt[:, 0:260],
                                                                 start=True, stop=False),
                                  r=["zt"], w=[psk(ob[0])])
                        else:
                            for b_ in ob:
                                P.add("pe", lambda h, b_=b_: h.matmul(PS[b_][:, 0:258], zt[:, 0:128], zt[:, 0:258],
                                                                     start=True, stop=False),
                                      r=["zt"], w=[psk(b_)])
                    for qb in range(off // 128, 4):
                        if vkind == "f":
                            last = (kb == 4 * j + 3 and qb == 3)
                            P.add("pe", lambda h, qb=qb, kb=kb, pts=pts, ob=ob, last=last, vidx=vidx: h.matmul(
                                PS[ob[0]][:, qb * 65:(qb + 1) * 65], PT[pts][:, qb * 128:(qb + 1) * 128],
                                Vf[:, kb, vidx, :], start=False, stop=last),
                                r=[("pt", pts), "V"], w=[psk(ob[0])])
                        else:
                            last = (kb == 4 * j + qb and qb % 2 == 1)
                            b_ = ob[qb // 2]
                            q2 = qb % 2
                            P.add("pe", lambda h, q2=q2, qb=qb, kb=kb, pts=pts, b_=b_, last=last: h.matmul(
                                PS[b_][:, q2 * 129:(q2 + 1) * 129], PT[pts][:, qb * 128:(qb + 1) * 128],
                                Vd[:, kb, :], start=False, stop=last),
                                r=[("pt", pts), "V"], w=[psk(b_)])
                    if kb == 4 * j + 3:
                        evac(j, ob)
            pvctr[0] += len(units)

        def fg_part1(l):
            P.add("pool", lambda h: h.dma_start(out=wfg[:].rearrange("p k c -> p (k c)"), in_=wfg_d[l]), w=["wfg"], dma=True)
            for tg in range(4):
                tsl = slice(tg * 512, (tg + 1) * 512)
                for kc in range(8):
                    P.add("pe", lambda h, kc=kc, tsl=tsl: h.matmul(PS[7][0:8, :], wfg[:, kc, :], hT[:, kc, tsl],
                                                                   start=(kc == 0), stop=(kc == 7)),
                          r=["wfg", ("hT", tg)], w=[psk(7)])
                P.add("act", lambda h, tsl=tsl: h.activation(out=cumv[0:8, tsl], in_=PS[7][0:8, :], func=AF.Exp,
                                                             scale=-1.0, bias=negbf[:, 0:1]),
                      r=[psk(7), "negbf"], w=["cum"])
            P.add("act", lambda h: h.activation(out=cumv[0:8, :], in_=cumv[0:8, :], func=AF.Ln, scale=1.0, bias=1.0),
                  r=["cum"], w=["cum"])

        def fg_scan(l):
            P.add("dve", lambda h: h.tensor_tensor_scan(out=cumv[0:8, :], data0=ones8[:, 0:1].broadcast_to([8, S]),
                                                        data1=cumv[0:8, :], initial=0.0, op0=ALU.mult, op1=ALU.add),
                  r=["cum", "ones8"], w=["cum"])

        ffn_pre = {}

        def ffn_blk_dma(l, blk):
            ws = wslot()
            P.add("pool", lambda h, ws=ws, blk=blk: h.dma_start(
                out=WS[ws][:].rearrange("p (a b) -> p a b", a=2),
                in_=wgu_d[l, blk].rearrange("p (a b) -> p a b", a=2)),
                w=[("w", ws)], dma=True)
            return ws

        def ffn_gateup(l, hh, modnext=False, hook=None):
            bctr = 0
            for blk in range(11):
                if modnext and 1 <= blk <= 8:
                    mod_blocks(l + 1, [blk - 1])
                if hook and blk in hook:
                    hook[blk]()
                if (l, hh, blk) in ffn_pre:
                    ws = ffn_pre.pop((l, hh, blk))
                else:
                    ws = ffn_blk_dma(l, blk)
                wv = WS[ws][:].rearrange("p (k c) -> p k c", k=8)
                for ci in range(2):
                    c = blk * 2 + ci
                    for tgl in range(2):
                        tg = 2 * hh + tgl
                        tsl = slice(tg * 512, (tg + 1) * 512)
                        gb = 2 * (bctr % 3)
                        bctr += 1
                        for kc in range(8):
                            P.add("pe", lambda h, kc=kc, gb=gb, ci=ci, tsl=tsl, wv=wv: h.matmul(
                                PS[gb][:], wv[:, kc, ci * 128:(ci + 1) * 128], hT[:, kc, tsl], start=(kc == 0), stop=(kc == 7)),
                                r=[("w", ws), ("hT", tg)], w=[psk(gb)])
                        for kc in range(8):
                            P.add("pe", lambda h, kc=kc, gb=gb, ci=ci, tsl=tsl, wv=wv: h.matmul(
                                PS[gb + 1][:], wv[:, kc, 256 + ci * 128:256 + (ci + 1) * 128], hT[:, kc, tsl],
                                start=(kc == 0), stop=(kc == 7)),
                                r=[("w", ws), ("hT", tg)], w=[psk(gb + 1)])
                        sg = SG[bctr % 2]
                        sgk = ("sg", bctr % 2)
                        P.add("act", lambda h, gb=gb, sg=sg: h.activation(out=sg[:], in_=PS[gb][:], func=AF.Silu),
                              r=[psk(gb)], w=[sgk])
                        P.add("dve", lambda h, gb=gb, sg=sg, c=c, tgl=tgl: h.tensor_tensor(
                            out=actT[:, c, tgl * 512:(tgl + 1) * 512], in0=sg[:], in1=PS[gb + 1][:], op=ALU.mult),
                            r=[psk(gb + 1), sgk], w=[("actT", c)])

        def ffn_down(l, hh, extra=None):
            extra = collections.deque(extra or ())
            for oc in range(8):
                wdi = oc % 2
                wdf = WD[wdi]
                wd = wdf.rearrange("p (c n) -> p c n", c=NFC)
                P.add("pool", lambda h, wdf=wdf, oc=oc: h.dma_start(
                    out=wdf.rearrange("p (a b) -> p a b", a=2), in_=wdn_d[l, oc].rearrange("p (a b) -> p a b", a=2)),
                    w=[("wd", wdi)] + WD_OVERLAP, dma=True)
                for tgl in range(2):
                    tg = 2 * hh + tgl
                    tsl = slice(tg * 512, (tg + 1) * 512)
                    yb = (5, 6, 7)[(oc * 2 + tgl) % 3]
                    for c in range(NFC):
                        P.add("pe", lambda h, yb=yb, c=c, wd=wd, tgl=tgl: h.matmul(
                            PS[yb][:], wd[:, c, :], actT[:, c, tgl * 512:(tgl + 1) * 512], start=(c == 0), stop=(c == NFC - 1)),
                            r=[("wd", wdi), ("actT", c)], w=[psk(yb)])
                    P.add("dve", lambda h, yb=yb, oc=oc, tsl=tsl: h.scalar_tensor_tensor(
                        out=xT[:, oc, tsl], in0=PS[yb][:], scalar=modT[l][:, 40 + oc:41 + oc], in1=xT[:, oc, tsl],
                        op0=ALU.mult, op1=ALU.add),
                        r=[psk(yb)] + mk(l, 40, 48), w=[("xT", tg), ("xTo", tg, oc)])
                    if l == depth - 1:
                        outs.append(P.add("sp", lambda h, oc=oc, tsl=tsl: h.dma_start(
                            out=outT_d[oc * 128:(oc + 1) * 128, tsl], in_=xT[:, oc, tsl]),
                            r=[("xTo", tg, oc)], dma=True))
                    if extra:
                        extra.popleft()()
            while extra:
                extra.popleft()()

        def layer(l):
            lam_init = 0.8 - 0.6 * math.exp(-0.3 * l)
            ppl = pp[:, l, :]
            P.add("dve", lambda h: h.tensor_copy(out=qkgs[:], in_=ppl[:, O_QKG:O_QKG + 4]), r=["pp"], w=["qkgs"])
            P.add("dve", lambda h: h.tensor_scalar(out=qkgs[:, 0:1], in0=ppl[:, O_QKG:O_QKG + 1], scalar1=0.125, scalar2=None, op0=ALU.mult),
                  r=["pp"], w=["qkgs"])
            P.add("dve", lambda h: h.tensor_scalar(out=qkgs[:, 2:3], in0=ppl[:, O_QKG + 2:O_QKG + 3], scalar1=0.125, scalar2=None, op0=ALU.mult),
                  r=["pp"], w=["qkgs"])
            P.add("dve", lambda h: h.tensor_scalar(out=negbf[:], in0=ppl[0:8, O_BF:O_BF + 1], scalar1=-1.0, scalar2=None, op0=ALU.mult),
                  r=["pp"], w=["negbf"])
            P.add("dve", lambda h: h.tensor_scalar(out=gnb[:], in0=ppl[:, O_GN:O_GN + 128], scalar1=(1.0 - lam_init), scalar2=None, op0=ALU.mult),
                  r=["pp"], w=["gnb"])
            for i in range(2):
                P.add("dve", lambda h, i=i: h.tensor_tensor(out=lamp[:], in0=ppl[:, O_LAM + 128 * i:O_LAM + 128 * i + 64],
                                                            in1=ppl[:, O_LAM + 128 * i + 64:O_LAM + 128 * i + 128], op=ALU.mult),
                      r=["pp"], w=["lamp"])
                P.add("dve", lambda h, i=i: h.tensor_reduce(out=lamt[:, i:i + 1], in_=lamp[:], axis=AX.X, op=ALU.add),
                      r=["lamp"], w=["lamt"])
            P.add("act", lambda h: h.activation(out=lamt[:, 2:4], in_=lamt[:, 0:2], func=AF.Exp), r=["lamt"], w=["lamt"])
            P.add("dve", lambda h: h.scalar_tensor_tensor(out=neglam[:], in0=lamt[:, 3:4], scalar=-lam_init, in1=lamt[:, 2:3],
                                                         op0=ALU.add, op1=ALU.subtract), r=["lamt"], w=["neglam"])


            P.barrier(BAR_ENGS)
            P.add("dve", lambda h: h.memset(Vf[:, :, :, 64:65], 1.0), w=["V"])
            P.add("dve", lambda h: h.memset(Vd[:, :, 128:129], 1.0), w=["V"])
            for s_ in range(2):
                P.add("pool", lambda h, s_=s_: h.dma_start(out=AUG[s_][0][68:72, :], in_=onesrow_d[0:4, :]),
                      w=[("aug", s_)] + FFN_KEYS, dma=True)
                P.add("pool", lambda h, s_=s_: h.dma_start(out=AUG[s_][1][64:68, :], in_=onesrow_d[4:8, :]),
                      w=[("aug", s_)] + FFN_KEYS, dma=True)
            fg_part1(l)

            def winp_dma(pidx):
                ws = wslot()
                P.add("pool", lambda h, ws=ws, pidx=pidx: h.dma_start(
                    out=WS[ws][:, 0:3072].rearrange("p (a b) -> p a b", a=2),
                    in_=winp_d[l, pidx].rearrange("p (a b) -> p a b", a=2)),
                    w=[("w", ws)], dma=True)
                return ws

            ws_next = winp_dma(0)
            for pidx in range(8):
                fox = pidx < 4
                pr = pidx if fox else pidx - 4
                ws = ws_next
                wv = WS[ws][:, 0:3072].rearrange("p (k c) -> p k c", k=8)
                combos = [(wi, tg) for wi in range(2) for tg in range(4)]
                if pidx == 0:
                    fg_scan(l)
                QL = 2
                for i in range(len(combos) + QL):
                    if pidx == 0 and i in (2, 4, 6, 8):
                        split_rows(8, cums_d, [i // 2 - 1])
                    if i >= QL:
                        ic = i - QL
                        wi, tg = combos[ic]
                        tsl = slice(tg * 512, (tg + 1) * 512)
                        ub = ic % 3
                        sq = SQ[ic % 2]
                        ptile, pkey = (pairq, "pairq") if wi == 0 else (pairk, "pairk")
                        gcol = (0 if fox else 2) + wi
                        tk = ic % 2
                        T = TMP[tk]
                        ssb = 5 + tk
                        P.add("pe", lambda h, sq=sq, ssb=ssb: h.matmul(PS[ssb][:], bdones, sq[:], start=True, stop=True),
                              r=[("sq", ic % 2), "cst"], w=[psk(ssb)])
                        P.add("act", lambda h, T=T, ssb=ssb: h.activation(out=T[:], in_=PS[ssb][:], func=AF.Ln, scale=1.0 / 64, bias=EPS),
                              r=[psk(ssb)], w=[("tmp", tk)])
                        P.add("act", lambda h, T=T: h.activation(out=T[:], in_=T[:], func=AF.Exp, scale=-0.5),
                              r=[("tmp", tk)], w=[("tmp", tk)])
                        P.add("dve", lambda h, ub=ub, tsl=tsl, ptile=ptile, gcol=gcol, T=T: h.scalar_tensor_tensor(
                            out=ptile[:, tsl], in0=PS[ub][:], scalar=qkgs[:, gcol:gcol + 1], in1=T[:],
                            op0=ALU.mult, op1=ALU.mult),
                            r=[psk(ub), ("tmp", tk), "qkgs"], w=[pkey])
                    if i < len(combos):
                        wi, tg = combos[i]
                        tsl = slice(tg * 512, (tg + 1) * 512)
                        ub = i % 3
                        for kc in range(8):
                            P.add("pe", lambda h, kc=kc, ub=ub, wi=wi, tsl=tsl, wv=wv: h.matmul(
                                PS[ub][:], wv[:, kc, wi * 128:(wi + 1) * 128], hT[:, kc, tsl], start=(kc == 0), stop=(kc == 7)),
                                r=[("w", ws), ("hT", tg)], w=[psk(ub)])
                        sq = SQ[i % 2]
                        P.add("act", lambda h, ub=ub, sq=sq: h.activation(out=sq[:], in_=PS[ub][:], func=AF.Square),
                              r=[psk(ub)], w=[("sq", i % 2)])
                for t4 in range(4):
                    vb = 3 + t4 % 2
                    for tt in range(4):
                        tok = (t4 * 4 + tt) * 128
                        for kc in range(8):
                            P.add("pe", lambda h, kc=kc, vb=vb, tt=tt, tok=tok, wv=wv: h.matmul(
                                PS[vb][:, tt * 128:(tt + 1) * 128], hT[:, kc, tok:tok + 128], wv[:, kc, 256:384],
                                start=(kc == 0), stop=(kc == 7)),
                                r=[("w", ws), ("hT", tok // 512)], w=[psk(vb)])
                    if fox:
                        P.add("dve", lambda h, vb=vb, t4=t4: h.tensor_copy(
                            out=Vf[:, t4 * 4:(t4 + 1) * 4, :, 0:64],
                            in_=PS[vb][:].rearrange("p (t h d) -> p t h d", t=4, h=2)),
                            r=[psk(vb)], w=["V"])
                    else:
                        P.add("dve", lambda h, vb=vb, t4=t4: h.tensor_copy(
                            out=Vd[:, t4 * 4:(t4 + 1) * 4, 0:128],
                            in_=PS[vb][:].rearrange("p (t d) -> p t d", t=4)),
                            r=[psk(vb)], w=["V"])
                for sub in range(2):
                    aslot = sub
                    Qa, Ka = AUG[aslot]
                    if fox:
                        head = 2 * pr + sub
                        bsrc = cums_d[head]
                        bkey = ("splitd", id(cums_d))
                    else:
                        head = pr
                        bsrc = alsp_d[head]
                        bkey = ("splitd", id(alsp_d))
                    P.add("sp", lambda h, Qa=Qa, sub=sub: h.dma_start(out=Qa[0:64, :], in_=pairq[sub * 64:(sub + 1) * 64, :]),
                          r=["pairq"], w=[("aug", aslot)], dma=True)
                    P.add("sp", lambda h, Ka=Ka, sub=sub: h.dma_start(out=Ka[0:64, :], in_=pairk[sub * 64:(sub + 1) * 64, :]),
                          r=["pairk"], w=[("aug", aslot)], dma=True)
                    P.add("sp", lambda h, Qa=Qa, bsrc=bsrc: h.dma_start(out=Qa[64:68, :], in_=bsrc),
                          r=[bkey], w=[("aug", aslot)], dma=True)
                    P.add("sp", lambda h, Ka=Ka, bsrc=bsrc: h.dma_start(out=Ka[68:72, :], in_=bsrc),
                          r=[bkey], w=[("aug", aslot)], dma=True)

                if pidx < 7:
                    ws_next = winp_dma(pidx + 1)
                mod_mis = [mod_dma(l, 8 + 2 * pidx), mod_dma(l, 9 + 2 * pidx)]
                if pidx in (3, 7):
                    ws2 = wslot()
                    wo = WS[ws2][:, 0:4096].rearrange("p (k c) -> p k c", k=4)
                    P.add("pool", lambda h, wo=wo, pidx=pidx: h.dma_start(
                        out=wo, in_=w_out_d[l][(0 if pidx == 3 else 1) * 512:((0 if pidx == 3 else 1) + 1) * 512, :].rearrange("(k p) c -> p k c", p=128)),
                        w=[("w", ws2)], dma=True)
                if pidx == 7:
                    ffn_pre[(l, 0, 0)] = ffn_blk_dma(l, 0)

                units = []
                for sub in range(2):
                    if fox:
                        head = 2 * pr + sub

                        def evac(j, ob, head=head):
                            O = PS[ob[0]][:, 0:260].rearrange("p (q d) -> p q d", q=4)
                            P.add("dve", lambda h, O=O: h.reciprocal(out=rc[:, 0:4], in_=O[:, :, 64]),
                                  r=[psk(ob[0])], w=["rc"])
                            for qb in range(4):
                                P.add("dve", lambda h, O=O, qb=qb, j=j: h.tensor_scalar(
                                    out=mixed[:, 4 * j + qb, head * 64:(head + 1) * 64], in0=O[:, qb, 0:64],
                                    scalar1=rc[:, qb:qb + 1], scalar2=None, op0=ALU.mult),
                                    r=[psk(ob[0]), "rc"], w=[("mixed", 4 * j + qb)])

                        units.append((sub, "f", sub, evac))
                    else:
                        head = pr
                        if sub == 0:
                            def evac(j, ob, head=head):
                                for qb in range(4):
                                    O = PS[ob[qb // 2]][:, 0:258].rearrange("p (q d) -> p q d", q=2)
                                    q2 = qb % 2
                                    P.add("dve", lambda h, O=O, q2=q2, qb=qb: h.reciprocal(out=rc[:, qb:qb + 1], in_=O[:, q2, 128:129]),
                                          r=[psk(ob[qb // 2])], w=["rc"])
                                    P.add("dve", lambda h, O=O, q2=q2, qb=qb, j=j: h.tensor_scalar(
                                        out=t1[:, 4 * j + qb, :], in0=O[:, q2, 0:128], scalar1=rc[:, qb:qb + 1], scalar2=None,
                                        op0=ALU.mult),
                                        r=[psk(ob[qb // 2]), "rc"], w=["cum"])
                        else:
                            def evac(j, ob, head=head):
                                for qb in range(4):
                                    O = PS[ob[qb // 2]][:, 0:258].rearrange("p (q d) -> p q d", q=2)
                                    q2 = qb % 2
                                    P.add("dve", lambda h, O=O, q2=q2, qb=qb: h.reciprocal(out=rc[:, qb:qb + 1], in_=O[:, q2, 128:129]),
                                          r=[psk(ob[qb // 2])], w=["rc"])
                                    P.add("dve", lambda h, qb=qb: h.tensor_scalar(
                                        out=rc[:, 4 + qb:5 + qb], in0=rc[:, qb:qb + 1], scalar1=neglam[:, 0:1], scalar2=None, op0=ALU.mult),
                                        r=["rc", "neglam"], w=["rc"])
                                    P.add("dve", lambda h, O=O, q2=q2, qb=qb, j=j: h.scalar_tensor_tensor(
                                        out=t1[:, 4 * j + qb, :], in0=O[:, q2, 0:128], scalar=rc[:, 4 + qb:5 + qb],
                                        in1=t1[:, 4 * j + qb, :], op0=ALU.mult, op1=ALU.add),
                                        r=[psk(ob[qb // 2]), "rc"], w=["cum"])
                                    P.add("dve", lambda h, qb=qb, j=j: h.scalar_tensor_tensor(
                                        out=junk[:], in0=t1[:, 4 * j + qb, :], scalar=1.0, in1=t1[:, 4 * j + qb, :],
                                        op0=ALU.mult, op1=ALU.mult, accum_out=ssd[:, qb:qb + 1]),
                                        r=["cum"], w=["ssd", "junk"])
                                P.add("dve", lambda h: h.tensor_scalar(out=ssd[:, 4:8], in0=ssd[:, 0:4], scalar1=1.0 / 128, scalar2=EPS,
                                                                       op0=ALU.mult, op1=ALU.add),
                                      r=["ssd"], w=["ssd"])
                                P.add("pool", lambda h: h.tensor_tensor(out=ssd[:, 4:8], in0=ssd[:, 4:8], in1=mhalf[:, 0:4], op=ALU.pow),
                                      r=["ssd", "mhalf"], w=["ssd"])
                                for qb in range(4):
                                    P.add("dve", lambda h, qb=qb, j=j: h.scalar_tensor_tensor(
                                        out=mixed[:, 4 * j + qb, head * 128:(head + 1) * 128], in0=t1[:, 4 * j + qb, :],
                                        scalar=ssd[:, 4 + qb:5 + qb], in1=gnb[:], op0=ALU.mult, op1=ALU.mult),
                                        r=["cum", "ssd", "gnb"], w=[("mixed", 4 * j + qb)])

                        units.append((sub, "d", 0, evac))

                attention_units(units)

                mod_blocks(l, [8 + 2 * pidx, 9 + 2 * pidx], mod_mis)

                if pidx in (3, 7):
                    hf = 0 if pidx == 3 else 1
                    xnb = [XN[0][:].bitcast(BF16), XN[1][:].bitcast(BF16)]
                    MT = [[PT[0], PT[1], PT[2], PT[3]],
                          [xnb[0][:, 0:512], xnb[0][:, 512:1024], xnb[1][:, 0:512], xnb[1][:, 512:1024]]]
                    MTK = [[("pt", 0), ("pt", 1), ("pt", 2), ("pt", 3)], [("xn", 0), ("xn", 0), ("xn", 1), ("xn", 1)]]
                    TB = [(0, 1), (5, 6)]

                    def op_transposes(tg):
                        par = tg % 2
                        for fc in range(4):
                            tb = TB[par][fc % 2]
                            tpv = PS[tb][:].bitcast(BF16)
                            for tt in range(4):
                                P.add("pe", lambda h, tpv=tpv, tt=tt, fc=fc, tg=tg: h.transpose(
                                    tpv[:, tt * 128:(tt + 1) * 128], mixed[:, 4 * tg + tt, fc * 128:(fc + 1) * 128], ident),
                                    r=[("mixed", 4 * tg + tt), "cst"], w=[psk(tb)])
                            mt = MT[par][fc]
                            if fc % 2 == 0:
                                P.add("act", lambda h, tpv=tpv, mt=mt: h.activation(out=mt, in_=tpv[:, 0:512], func=AF.Copy),
                                      r=[psk(tb)], w=[MTK[par][fc]])
                            else:
                                P.add("dve", lambda h, tpv=tpv, mt=mt: h.tensor_copy(out=mt, in_=tpv[:, 0:512]),
                                      r=[psk(tb)], w=[MTK[par][fc]])

                    op_transposes(0)
                    for tg in range(4):
                        tsl = slice(tg * 512, (tg + 1) * 512)
                        par = tg % 2
                        if tg + 1 < 4:
                            op_transposes(tg + 1)
                        for oc in range(8):
                            yb = 2 + oc % 3
                            for fc in range(4):
                                P.add("pe", lambda h, yb=yb, fc=fc, oc=oc, wo=wo, par=par: h.matmul(
                                    PS[yb][:], wo[:, fc, oc * 128:(oc + 1) * 128], MT[par][fc], start=(fc == 0), stop=(fc == 3)),
                                    r=[("w", ws2), MTK[par][fc]], w=[psk(yb)])
                            P.add("dve", lambda h, yb=yb, oc=oc, tsl=tsl: h.scalar_tensor_tensor(
                                out=xT[:, oc, tsl], in0=PS[yb][:], scalar=modT[l][:, 16 + oc:17 + oc], in1=xT[:, oc, tsl],
                                op0=ALU.mult, op1=ALU.add),
                                r=[psk(yb)] + mk(l, 16, 24), w=[("xT", tg)])
                        if hf == 1 and tg < 2:
                            rms_p1(tg)

            mod_a(l, 2)
            P.barrier(BAR_ENGS)
            nxt = l + 1 < depth
            rms_p2(l, 0, a2[l], ("a2", l), 24)
            rms_p2(l, 1, a2[l], ("a2", l), 24)
            ffn_gateup(l, 0, modnext=nxt, hook={6: lambda: rms_p1(2), 8: lambda: rms_p1(3)})
            if nxt:
                mod_a(l + 1, 1)
            ffn_down(l, 0, extra=rms_p2_thunks(l, 2, a2[l], ("a2", l), 24) + rms_p2_thunks(l, 3, a2[l], ("a2", l), 24))
            ffn_gateup(l, 1, hook=({6: lambda: rms_p1(0), 8: lambda: rms_p1(1)} if nxt else None))
            ffn_down(l, 1, extra=(rms_p2_thunks(l + 1, 0, a1[l + 1], ("a1", l + 1), 0)
                                  + rms_p2_thunks(l + 1, 1, a1[l + 1], ("a1", l + 1), 0)) if nxt else None)
            if nxt:
                rms_seq(l + 1, (2, 3), a1[l + 1], ("a1", l + 1), 0)

        outs = []
        prologue()
        for l in range(depth):
            layer(l)

        ops = P.emit(sems, dsems)
        for oi in outs:
            op = ops[oi]
            nc.sync.wait_ge(op["dsem"], op["dval"])
    return nc


_NC_CACHE = {}


def _host_inputs(inp):
    f = lambda a: np.ascontiguousarray(np.asarray(a, dtype=np.float32))
    x = f(inp["x"]); c = f(inp["c"])
    B = x.shape[0]
    pp = np.zeros((DEPTH, 128, NP_), np.float32)
    for l in range(DEPTH):
        pp[l, :, O_LN1:O_LN1 + 8] = f(inp["ln1_g"])[l].reshape(8, 128).T
        pp[l, :, O_LN2:O_LN2 + 8] = f(inp["ln2_g"])[l].reshape(8, 128).T
        pp[l, :, O_BADA:O_BADA + 48] = f(inp["b_ada"])[l].reshape(48, 128).T
        pp[l, :, O_QKG + 0] = np.tile(f(inp["fox_qk_g"])[l, 0], 2)
        pp[l, :, O_QKG + 1] = np.tile(f(inp["fox_qk_g"])[l, 1], 2)
        pp[l, :, O_QKG + 2] = np.tile(f(inp["diff_qk_g"])[l, 0], 2)
        pp[l, :, O_QKG + 3] = np.tile(f(inp["diff_qk_g"])[l, 1], 2)
        pp[l, :, O_LAM:O_LAM + 256] = np.broadcast_to(f(inp["diff_lam"])[l].reshape(1, 256), (128, 256))
        pp[l, :, O_GN:O_GN + 128] = np.broadcast_to(f(inp["diff_norm_g"])[l].reshape(1, 128), (128, 128))
        pp[l, 0:8, O_BF] = f(inp["b_f"])[l]
    cst = np.zeros((128, 512), np.float32)
    cst[:, 0:128] = np.eye(128, dtype=np.float32)
    kk = np.arange(128)[:, None]; qq = np.arange(128)[None, :]
    cst[:, 128:256] = np.where(kk <= qq, 0.0, -30000.0)
    cst[0:64, 256:320] = 1.0
    cst[64:128, 320:384] = 1.0
    cst[:, 384:512] = 1.0
    slopes = np.array([2.0 ** (-8.0 * (h + 1) / 4) for h in range(4)], np.float32)
    import ml_dtypes
    al = (slopes[:, None] * np.arange(S, dtype=np.float32)[None, :]).astype(np.float32)
    parts = []
    rem = al.copy()
    for _ in range(4):
        p_ = rem.astype(ml_dtypes.bfloat16).astype(np.float32)
        parts.append(p_)
        rem = (rem - p_).astype(np.float32)
    alibi = np.ascontiguousarray(np.stack(parts, 1))
    onesrow = np.concatenate([np.ones((4, S), np.float32), -np.ones((4, S), np.float32)], 0)
    w_ada = f(inp["w_ada"]); w_in = f(inp["w_in"]); w_gate = f(inp["w_gate"]); w_up = f(inp["w_up"]); w_down = f(inp["w_down"])
    wada = np.ascontiguousarray(w_ada.reshape(DEPTH, 8, 128, 24, 256).transpose(0, 3, 2, 1, 4)).reshape(DEPTH, 24, 128, 2048)
    winp = np.empty((DEPTH, 8, 128, 8, 384), np.float32)
    for pidx in range(8):
        base = 0 if pidx < 4 else 1544
        pr = pidx % 4
        for wi in range(3):
            c0 = base + wi * 512 + pr * 128
            winp[:, pidx, :, :, wi * 128:(wi + 1) * 128] = w_in[:, :, c0:c0 + 128].reshape(DEPTH, 8, 128, 128).transpose(0, 2, 1, 3)
    winp = winp.reshape(DEPTH, 8, 128, 3072)
    wfg = np.ascontiguousarray(w_in[:, :, 1536:1544].reshape(DEPTH, 8, 128, 8).transpose(0, 2, 1, 3)).reshape(DEPTH, 128, 64)
    wgu = np.empty((DEPTH, 11, 128, 8, 512), np.float32)
    wgu[..., 0:256] = w_gate.reshape(DEPTH, 8, 128, 11, 256).transpose(0, 3, 2, 1, 4)
    wgu[..., 256:512] = w_up.reshape(DEPTH, 8, 128, 11, 256).transpose(0, 3, 2, 1, 4)
    wgu = wgu.reshape(DEPTH, 11, 128, 4096)
    wdn = np.ascontiguousarray(w_down.reshape(DEPTH, NFC, 128, 8, 128).transpose(0, 3, 2, 1, 4)).reshape(DEPTH, 8, 128, DFF)
    shared = dict(pp=pp, cst=cst, alibi=alibi, onesrow=onesrow, wada=wada, winp=winp, wfg=wfg,
                  w_out=f(inp["w_out"]), wgu=wgu, wdn=wdn)
    maps = []
    for b in range(B):
        m = dict(shared)
        m["xT"] = np.ascontiguousarray(x[b].T)
        m["cT"] = np.ascontiguousarray(c[b].reshape(8, 128).T)
        maps.append(m)
    return maps


def kernel(**inputs):
    maps = _host_inputs(inputs)
    if "nc" not in _NC_CACHE:
        _NC_CACHE["nc"] = build()
    nc = _NC_CACHE["nc"]
    res = run_bass_kernel_spmd(nc, maps, core_ids=list(range(len(maps))))
    out = np.stack([np.ascontiguousarray(np.asarray(r["outT"], dtype=np.float32).T) for r in res.results], 0)
    return out
```
